# Optimizing a Trainium2 kernel written in Bass

```python
import jax, jax.numpy as jnp
from jax import lax
import numpy as np

D_MODEL = 1024
BATCH = 4
SEQ = 4096
DEPTH = 2

MEM_LEN = 256
D_MIX = 2 * D_MODEL
D_GROUP = D_MIX // 4
A_HEADS = 4
A_HD = D_GROUP // A_HEADS
B_HEADS = 8
B_KV_HEADS = 2
B_HD = D_GROUP // B_HEADS
WINDOW = 128
BLOCK = 128
ROPE_THETA = 500000.0
ROPE_DIM = B_HD // 4
C_HEADS = 4
C_QK = D_GROUP // (2 * C_HEADS)
C_V = D_GROUP // C_HEADS
CONV_W = 5
D_HEADS = 4
D_HD = D_GROUP // D_HEADS

CHUNK = 64
MAX_POS_OFFSET = 1024
EPS = 1e-6

IN_SIZES = (
    D_GROUP, D_GROUP, D_GROUP, D_GROUP, D_GROUP,
    B_HEADS * B_HD, B_KV_HEADS * B_HD, B_KV_HEADS * B_HD, D_GROUP,
    C_HEADS * C_QK, C_HEADS * C_QK, C_HEADS * C_V, D_GROUP, D_GROUP,
    2 * C_HEADS, 2 * C_HEADS,
    D_GROUP, D_GROUP,
)
D_IN = sum(IN_SIZES)
SPLIT_POINTS = tuple(int(s) for s in np.cumsum(IN_SIZES)[:-1])

kernel_name = "hybrid_hgrn2_swa_mlstm_memxattn_encoder"

F32 = jnp.float32


def _rmsnorm(x, g):
    xf = x.astype(F32)
    y = xf * lax.rsqrt(jnp.mean(xf * xf, axis=-1, keepdims=True) + EPS)
    return (y * g.astype(F32)).astype(x.dtype)


def _head_layernorm(h, g):
    mu = jnp.mean(h, axis=-1, keepdims=True)
    var = jnp.mean(jnp.square(h - mu), axis=-1, keepdims=True)
    return (h - mu) * lax.rsqrt(var + EPS) * g.astype(F32)


def _to_chunks(a):
    b, h, t = a.shape[:3]
    a = a.reshape((b, h, t // CHUNK, CHUNK) + a.shape[3:])
    return jnp.moveaxis(a, 2, 0)


def _from_chunks(a):
    a = jnp.moveaxis(a, 0, 2)
    return a.reshape(a.shape[:2] + (a.shape[2] * a.shape[3],) + a.shape[4:])


def _flip_t(a):
    return jnp.flip(a, axis=2)


def _hgrn2_scan(q, k, v, log_f):
    bsz, nh, _, dk = q.shape
    dv = v.shape[-1]
    mask = jnp.tril(jnp.ones((CHUNK, CHUNK), bool))[:, :, None]

    def step(S, inp):
        qb, kb, vb, gb = inp
        b = jnp.cumsum(gb, axis=2)
        diff = b[:, :, :, None, :] - b[:, :, None, :, :]
        decay = jnp.exp(jnp.where(mask, diff, -jnp.inf))
        attn = jnp.einsum('bhtd,bhsd,bhtsd->bhts', qb, kb, decay)
        o = (jnp.einsum('bhts,bhsv->bhtv', attn, vb)
             + jnp.einsum('bhtd,bhdv->bhtv', qb * jnp.exp(b), S))
        b_last = b[:, :, -1:, :]
        S = (jnp.exp(b_last[:, :, 0, :])[..., None] * S
             + jnp.einsum('bhsd,bhsv->bhdv', kb * jnp.exp(b_last - b), vb))
        return S, o

    S0 = jnp.zeros((bsz, nh, dk, dv), F32)
    _, o = lax.scan(step, S0, (_to_chunks(q), _to_chunks(k), _to_chunks(v), _to_chunks(log_f)))
    return _from_chunks(o)


def _hgrn2_branch(q, i, zf, zb, z, lb, norm_g):
    bsz, t, _ = q.shape
    heads = lambda a: a.astype(F32).reshape(bsz, t, A_HEADS, A_HD).transpose(0, 2, 1, 3)
    lbh = lb.astype(F32).reshape(A_HEADS, 1, A_HD)
    qh, vh = heads(q), heads(i)

    def gates(zg):
        zg = heads(zg)
        f = lbh + (1.0 - lbh) * jax.nn.sigmoid(zg)
        return (1.0 - lbh) * jax.nn.sigmoid(-zg), jnp.log(f)

    kf, gf = gates(zf)
    kb, gb = gates(zb)
    o_f = _hgrn2_scan(qh, kf, vh, gf)
    o_b = _flip_t(_hgrn2_scan(_flip_t(qh), _flip_t(kb), _flip_t(vh), _flip_t(gb)))
    o = (o_f + o_b).transpose(0, 2, 1, 3)
    o = _rmsnorm(o, norm_g.reshape(A_HEADS, A_HD))
    return o.reshape(bsz, t, D_GROUP).astype(z.dtype) * jax.nn.silu(z)


def _rope(x, cos, sin):
    half = ROPE_DIM // 2
    x1 = x[..., :half].astype(F32)
    x2 = x[..., half:ROPE_DIM].astype(F32)
    rot = jnp.concatenate([x1 * cos - x2 * sin, x2 * cos + x1 * sin], axis=-1).astype(x.dtype)
    return jnp.concatenate([rot, x[..., ROPE_DIM:]], axis=-1)


def _window_attention(q, k, v, z, sink, cos, sin):
    bsz, t, _ = q.shape
    nb = t // BLOCK
    grp = B_HEADS // B_KV_HEADS
    q = _rope(q.reshape(bsz, t, B_HEADS, B_HD), cos, sin).reshape(bsz, nb, BLOCK, B_KV_HEADS, grp, B_HD)
    k = _rope(k.reshape(bsz, t, B_KV_HEADS, B_HD), cos, sin)
    v = v.reshape(bsz, t, B_KV_HEADS, B_HD)

    def band(a):
        a = jnp.pad(a, ((0, 0), (BLOCK, BLOCK), (0, 0), (0, 0))).reshape(bsz, nb + 2, BLOCK, B_KV_HEADS, B_HD)
        return jnp.concatenate([a[:, :-2], a[:, 1:-1], a[:, 2:]], axis=2)

    kw, vw = band(k), band(v)
    qpos = jnp.arange(t).reshape(nb, BLOCK)
    kpos = jnp.arange(-BLOCK, t + BLOCK).reshape(nb + 2, BLOCK)
    kpos = jnp.concatenate([kpos[:-2], kpos[1:-1], kpos[2:]], axis=1)
    kp = kpos[:, None, :]
    valid = (jnp.abs(qpos[:, :, None] - kp) <= WINDOW) & (kp >= 0) & (kp < t)
    s = jnp.einsum('bnqkgd,bnskd->bnkgqs', q, kw).astype(F32) * (B_HD ** -0.5)
    s = jnp.where(valid[None, :, None, None], s, -jnp.inf)
    sk = jnp.broadcast_to(sink.astype(F32).reshape(1, 1, B_KV_HEADS, grp, 1, 1), s.shape[:-1] + (1,))
    p = jax.nn.softmax(jnp.concatenate([s, sk], axis=-1), axis=-1)[..., :-1]
    o = jnp.einsum('bnkgqs,bnskd->bnqkgd', p.astype(v.dtype), vw)
    return o.reshape(bsz, t, D_GROUP) * jax.nn.silu(z)


def _centered_dwconv(x, w):
    pad = CONV_W // 2
    return lax.conv_general_dilated(
        x, w[:, None, :].astype(x.dtype), window_strides=(1,), padding=((pad, pad),),
        dimension_numbers=('NWC', 'WIO', 'NWC'), feature_group_count=x.shape[-1])


def _mlstm_scan(q, k, v, log_i, log_f):
    bsz, nh, _, dk = q.shape
    dv = v.shape[-1]
    mask = jnp.tril(jnp.ones((CHUNK, CHUNK), bool))

    def step(carry, inp):
        C, n, m = carry
        qb, kb, vb, ib, fb = inp
        b = jnp.cumsum(fb, axis=-1)
        d_intra = jnp.where(mask, b[..., :, None] - b[..., None, :] + ib[..., None, :], -jnp.inf)
        d_inter = b + m[..., None]
        m_t = jnp.maximum(jnp.max(d_intra, axis=-1), d_inter)
        w = jnp.einsum('bhtd,bhsd->bhts', qb, kb) * jnp.exp(d_intra - m_t[..., None])
        g = jnp.exp(d_inter - m_t)
        num = jnp.einsum('bhts,bhsv->bhtv', w, vb) + g[..., None] * jnp.einsum('bhtd,bhdv->bhtv', qb, C)
        den = jnp.sum(w, axis=-1) + g * jnp.einsum('bhtd,bhd->bht', qb, n)
        h = num / jnp.maximum(jnp.abs(den), jnp.exp(-m_t))[..., None]
        b_last = b[..., -1]
        a = b_last[..., None] - b + ib
        m_new = jnp.maximum(b_last + m, jnp.max(a, axis=-1))
        ws = jnp.exp(a - m_new[..., None])
        decay = jnp.exp(b_last + m - m_new)
        C = decay[..., None, None] * C + jnp.einsum('bhs,bhsd,bhsv->bhdv', ws, kb, vb)
        n = decay[..., None] * n + jnp.einsum('bhs,bhsd->bhd', ws, kb)
        return (C, n, m_new), h

    init = (jnp.zeros((bsz, nh, dk, dv), F32), jnp.zeros((bsz, nh, dk), F32), jnp.zeros((bsz, nh), F32))
    _, h = lax.scan(step, init, tuple(_to_chunks(a) for a in (q, k, v, log_i, log_f)))
    return _from_chunks(h)


def _mlstm_branch(q, k, v, o, z, ig, fg, gate_b, conv_w, norm_g):
    bsz, t, _ = q.shape
    qk = jax.nn.silu(_centered_dwconv(jnp.concatenate([q, k], axis=-1), conv_w))
    q, k = jnp.split(qk, 2, axis=-1)
    heads = lambda a, d: a.astype(F32).reshape(bsz, t, C_HEADS, d).transpose(0, 2, 1, 3)
    qh = heads(q, C_QK) * (C_QK ** -0.5)
    kh = heads(k, C_QK)
    vh = heads(v, C_V)
    gb = gate_b.astype(F32)
    to_dir = lambda a: a.reshape(bsz, t, 2, C_HEADS).transpose(2, 0, 3, 1)
    log_i = to_dir(ig.astype(F32) + gb[:2 * C_HEADS])
    log_f = to_dir(jax.nn.log_sigmoid(fg.astype(F32) + gb[2 * C_HEADS:]))
    h_f = _mlstm_scan(qh, kh, vh, log_i[0], log_f[0])
    h_b = _flip_t(_mlstm_scan(_flip_t(qh), _flip_t(kh), _flip_t(vh), _flip_t(log_i[1]), _flip_t(log_f[1])))
    h = (h_f + h_b).transpose(0, 2, 1, 3) * jax.nn.sigmoid(o.astype(F32)).reshape(bsz, t, C_HEADS, C_V)
    h = _head_layernorm(h, norm_g.reshape(C_HEADS, C_V))
    return h.reshape(bsz, t, D_GROUP).astype(z.dtype) * jax.nn.silu(z)


def _memory_attention(q, z, mem_n, w_kv):
    bsz, t, _ = q.shape
    m = mem_n.shape[1]
    k, v = jnp.split(mem_n @ w_kv, 2, axis=-1)
    k = k.reshape(bsz, m, D_HEADS, D_HD)
    v = v.reshape(bsz, m, D_HEADS, D_HD)
    qh = q.reshape(bsz, t, D_HEADS, D_HD)
    s = jnp.einsum('bthd,bmhd->bhtm', qh, k).astype(F32) * (D_HD ** -0.5)
    p = jax.nn.softmax(s, axis=-1).astype(v.dtype)
    o = jnp.einsum('bhtm,bmhd->bthd', p, v).reshape(bsz, t, D_GROUP)
    return o * jax.nn.silu(z)


def setup_inputs(seed: int = 0) -> dict:
    key = jax.random.key(seed)
    ks = jax.random.split(key, 18)
    nrm = lambda k, shape: jax.random.normal(k, shape, F32)
    x = nrm(ks[0], (BATCH, SEQ, D_MODEL))
    mem = nrm(ks[1], (BATCH, MEM_LEN, D_MODEL))
    positions = (jax.random.randint(ks[2], (BATCH, 1), 0, MAX_POS_OFFSET)
                 + jnp.arange(SEQ)[None, :]).astype(jnp.int32)
    norm_g = 1.0 + 0.02 * nrm(ks[3], (DEPTH, D_MODEL))
    w_in = nrm(ks[4], (DEPTH, D_MODEL, D_IN)) * (D_MODEL ** -0.5)
    hgrn_lb_logits = 0.5 * nrm(ks[5], (DEPTH, D_GROUP))
    hgrn_norm_g = 1.0 + 0.02 * nrm(ks[6], (DEPTH, D_GROUP))
    attn_sink = 0.5 * nrm(ks[7], (DEPTH, B_HEADS))
    mlstm_conv_w = nrm(ks[8], (DEPTH, CONV_W, 2 * C_HEADS * C_QK)) * (CONV_W ** -0.5)
    f_bias = jnp.tile(jnp.linspace(3.0, 6.0, C_HEADS), 2)[None, :] + 0.1 * nrm(ks[9], (DEPTH, 2 * C_HEADS))
    i_bias = 0.1 * nrm(ks[10], (DEPTH, 2 * C_HEADS))
    mlstm_gate_b = jnp.concatenate([i_bias, f_bias], axis=-1)
    mlstm_norm_g = 1.0 + 0.02 * nrm(ks[11], (DEPTH, D_GROUP))
    mem_norm_g = 1.0 + 0.02 * nrm(ks[12], (DEPTH, D_MODEL))
    w_mem_kv = nrm(ks[13], (DEPTH, D_MODEL, 2 * D_GROUP)) * (D_MODEL ** -0.5)
    w_out = nrm(ks[14], (DEPTH, D_MIX, D_MODEL)) * (D_MIX ** -0.5)
    final_norm_g = 1.0 + 0.02 * nrm(ks[15], (D_MODEL,))
    return {
        "x": x, "mem": mem, "positions": positions, "norm_g": norm_g, "w_in": w_in,
        "hgrn_lb_logits": hgrn_lb_logits, "hgrn_norm_g": hgrn_norm_g, "attn_sink": attn_sink,
        "mlstm_conv_w": mlstm_conv_w, "mlstm_gate_b": mlstm_gate_b, "mlstm_norm_g": mlstm_norm_g,
        "mem_norm_g": mem_norm_g, "w_mem_kv": w_mem_kv, "w_out": w_out, "final_norm_g": final_norm_g,
    }


def reference(x, mem, positions, norm_g, w_in, hgrn_lb_logits, hgrn_norm_g, attn_sink,
              mlstm_conv_w, mlstm_gate_b, mlstm_norm_g, mem_norm_g, w_mem_kv, w_out, final_norm_g):
    lb_all = jnp.cumsum(jax.nn.softmax(hgrn_lb_logits.astype(F32), axis=0), axis=0)
    lb_all = lb_all - lb_all[:1]
    inv_freq = ROPE_THETA ** (-jnp.arange(0, ROPE_DIM, 2, dtype=F32) / ROPE_DIM)
    ang = positions.astype(F32)[..., None] * inv_freq
    cos = jnp.cos(ang)[:, :, None, :]
    sin = jnp.sin(ang)[:, :, None, :]
    for l in range(DEPTH):
        h = _rmsnorm(x, norm_g[l])
        u = h @ w_in[l]
        (a_q, a_i, a_ff, a_fb, a_z,
         b_q, b_k, b_v, b_z,
         c_q, c_k, c_v, c_o, c_z, c_ig, c_fg,
         d_q, d_z) = jnp.split(u, SPLIT_POINTS, axis=-1)
        y_a = _hgrn2_branch(a_q, a_i, a_ff, a_fb, a_z, lb_all[l], hgrn_norm_g[l])
        y_b = _window_attention(b_q, b_k, b_v, b_z, attn_sink[l], cos, sin)
        y_c = _mlstm_branch(c_q, c_k, c_v, c_o, c_z, c_ig, c_fg, mlstm_gate_b[l], mlstm_conv_w[l], mlstm_norm_g[l])
        y_d = _memory_attention(d_q, d_z, _rmsnorm(mem, mem_norm_g[l]), w_mem_kv[l])
        y = jnp.concatenate([y_a, y_b, y_c, y_d], axis=-1)
        x = x + y @ w_out[l]
    return _rmsnorm(x, final_norm_g)
```

```python
import math
from concourse.bass_utils import run_bass_kernel_spmd
import numpy as np
import concourse.bass as bass
import concourse.mybir as mybir

F32 = mybir.dt.float32
BF16 = mybir.dt.bfloat16
I32 = mybir.dt.int32
ALU = mybir.AluOpType
AF = mybir.ActivationFunctionType
AX = mybir.AxisListType

NDS = 40


class Buf:
    def __init__(self, t, name):
        self.t = t
        self.name = name
        self.w = None
        self.r = {}

    def __getitem__(self, idx):
        return self.t[idx]


class Ctx:
    def __init__(self, nc):
        self.nc = nc
        self.E = {"pe": nc.tensor, "act": nc.scalar, "dve": nc.vector, "pool": nc.gpsimd, "sp": nc.sync}
        self.sem = {k: nc.alloc_semaphore("s_" + k) for k in self.E}
        self.cnt = {k: 0 for k in self.E}
        self.seen = {k: {} for k in self.E}
        self.dsem = [nc.alloc_semaphore(f"d{i}") for i in range(NDS)]
        self.dval = [0] * NDS
        self.di = 0
        self.guards = []
        self.nops = 0

    def sb(self, name, shape, dt=F32):
        self.uid = getattr(self, "uid", 0) + 1
        name = f"{name}_{self.uid}"
        g = self.nc.sbuf_tensor(name, list(shape), dt)
        t = g.__enter__()
        self.guards.append(g)
        return Buf(t, name)

    def ps(self, name, shape, dt=F32):
        self.uid = getattr(self, "uid", 0) + 1
        name = f"{name}_{self.uid}"
        g = self.nc.psum_tensor(name, list(shape), dt)
        t = g.__enter__()
        self.guards.append(g)
        return Buf(t, name)

    def dram(self, name, shape, dt=F32, kind="Internal"):
        t = self.nc.dram_tensor(name, list(shape), dt, kind=kind)
        return Buf(t, name)

    def _wait(self, eng, tok):
        kind, key, val = tok
        k = (kind, key)
        if self.seen[eng].get(k, 0) >= val:
            return
        semh = self.sem[key] if kind == "e" else (self.dsem[key] if kind == "d" else self.csem[key])
        self.E[eng].wait_ge(semh, val)
        self.seen[eng][k] = val

    def _deps(self, eng, w, r):
        toks = []
        for b in r:
            if b.w is not None:
                toks.append(b.w)
        for b in w:
            if b.w is not None:
                toks.append(b.w)
            for k, v in b.r.items():
                toks.append((k[0], k[1], v))
        for tok in toks:
            if eng == "pe" and tok[0] == "e" and tok[1] == "pe":
                continue
            self._wait(eng, tok)

    def _mark(self, tok, w, r):
        k = (tok[0], tok[1])
        for b in r:
            if b.r.get(k, 0) < tok[2]:
                b.r[k] = tok[2]
        for b in w:
            b.w = tok
            b.r = {}

    def op(self, eng, fn, w=(), r=()):
        self._deps(eng, w, r)
        ins = fn(self.E[eng])
        self.cnt[eng] += 1
        ins.then_inc(self.sem[eng], 1)
        tok = ("e", eng, self.cnt[eng])
        self._mark(tok, w, r)
        self.nops += 1
        return tok

    def dma(self, q, out_ap, in_ap, w=(), r=(), **kw):
        import os
        if os.environ.get("ALLSP"):
            q = "sp"
        self._deps(q, w, r)
        slot = self.di % NDS
        self.di += 1
        if self.dval[slot] > 0:
            self._wait(q, ("d", slot, self.dval[slot]))
        self.dval[slot] += 16
        self.E[q].dma_start(out=out_ap, in_=in_ap, **kw).then_inc(self.dsem[slot], 16)
        tok = ("d", slot, self.dval[slot])
        self._mark(tok, w, r)
        self.nops += 1
        return tok

    def collective(self, kind, alu, groups, in_buf, out_buf):
        if not hasattr(self, "csem"):
            self.csem = []
        self._deps("pool", [out_buf], [in_buf])
        sem = self.nc.alloc_semaphore(f"cc{len(self.csem)}")
        self.csem.append(sem)
        self.nc.gpsimd.collective_compute(kind, alu, replica_groups=groups, ins=[in_buf.t.ap().opt()], outs=[out_buf.t.ap().opt()]).then_inc(sem)
        tok = ("c", len(self.csem) - 1, 1)
        self._mark(tok, [out_buf], [in_buf])
        return tok

    def finish(self, bufs, eng="sp"):
        for b in bufs:
            if b.w is not None:
                self._wait(eng, b.w)

    def close(self):
        for g in reversed(self.guards):
            g.__exit__(None, None, None)
        self.guards = []


D = 1024
EPS = 1e-6


class Cfg:
    def __init__(self, T, HA, GB, HC, HD, L=2, MEM=256):
        self.T, self.HA, self.GB, self.HC, self.HD, self.L, self.MEM = T, HA, GB, HC, HD, L, MEM
        self.NT = T // 128
        self.NB = T // 512
        cols = [("A_q", HA * 128), ("A_i", HA * 128), ("A_ff", HA * 128), ("A_fb", HA * 128), ("A_z", HA * 128),
                ("B_q", GB * 256), ("B_k", GB * 64), ("B_v", GB * 64), ("B_z", GB * 256),
                ("C_q", HC * 64), ("C_k", HC * 64), ("C_v", HC * 128), ("C_o", HC * 128), ("C_z", HC * 128),
                ("C_g", 4 * HC), ("D_q", HD * 128), ("D_z", HD * 128)]
        self.col = {}
        o = 0
        for n, w in cols:
            self.col[n] = (o, w)
            o += w
        self.NIN = o
        self.yA, self.yB, self.yC, self.yD = 0, HA * 128, HA * 128 + GB * 256, HA * 128 + GB * 256 + HC * 128
        self.NY = self.yD + HD * 128
        self.pp = {}
        o = 0
        for l in range(L):
            for h in range(HA):
                self.pp[("A", l, h)] = o
                o += 3
            for p in range(HC // 2):
                self.pp[("Cw", l, p)] = o
                o += 10
        self.NPP = o
        self.rp = {}
        o = 0
        for l in range(L):
            for n, w in (("gC", HC * 128), ("gb", 4 * HC), ("sink", GB * 4)):
                self.rp[(n, l)] = (o, w)
                o += w
        self.NRS = o
        for l in range(L):
            for n, w in (("ng", D), ("mg", D)):
                self.rp[(n, l)] = (o, w)
                o += w
        self.rp[("fg", 0)] = (o, D)
        o += D
        self.NRP = o
        self.cc = {}
        o = 0
        for n, w in (("ident", 128), ("triU", 128), ("triL", 128), ("bdU", 128), ("bdL", 128), ("P64T", 64),
                     ("invf", 1), ("scanm", 512), ("ones", 128)):
            self.cc[n] = (o, w)
            o += w
        self.NCC = o


def make_consts(cfg):
    c = np.zeros((128, cfg.NCC), np.float32)

    def put(n, a):
        o, w = cfg.cc[n]
        c[: a.shape[0], o:o + w] = a

    s = np.arange(128)[:, None]
    t = np.arange(128)[None, :]
    put("ident", np.eye(128, dtype=np.float32))
    put("triU", (s <= t).astype(np.float32))
    put("triL", (s >= t).astype(np.float32))
    same = (s // 64) == (t // 64)
    put("bdU", ((s <= t) & same).astype(np.float32))
    put("bdL", ((s >= t) & same).astype(np.float32))
    P = np.zeros((64, 64), np.float32)
    for i in range(8):
        P[i + 8, i] = -1.0
        P[i, i + 8] = 1.0
    put("P64T", P)
    invf = (500000.0 ** (-np.arange(0, 16, 2, dtype=np.float32) / 16.0)).astype(np.float32)
    iv = np.zeros((128, 1), np.float32)
    iv[0:8, 0] = invf / np.float32(2 * np.pi)
    iv[8:16, 0] = invf / np.float32(2 * np.pi)
    put("invf", iv)
    sm = np.ones((128, 512), np.float32)
    sm[:, ::64] = 0.0
    put("scanm", sm)
    put("ones", np.ones((128, 128), np.float32))
    return c


class BfView:
    def __init__(self, b):
        object.__setattr__(self, "_b", b)

    def __getitem__(self, idx):
        return self._b.t[:].bitcast(BF16)[idx]

    def __getattr__(self, k):
        return getattr(self._b, k)

    def __setattr__(self, k, v):
        setattr(self._b, k, v)


def pipeline(n, stages):
    S = len(stages)
    for step in range(n + S - 1):
        for s_ in range(S - 1, -1, -1):
            i = step - s_
            if 0 <= i < n:
                stages[s_](i)


class Gen:
    def __init__(self, cfg, phases="NDBACO", mode="full", layers=None, debug_out=None):
        self.cfg = cfg
        self.phases = phases
        self.mode = mode
        self.layers = list(range(cfg.L)) if layers is None else layers
        self.debug_out = debug_out
        nc = bass.Bass("TRN2", target_bir_lowering=False)
        self.nc = nc
        self.c = Ctx(nc)

    def barrier(self):
        c = self.c
        for e in ("pe", "act", "dve", "pool", "sp"):
            for o in ("pe", "act", "dve", "pool"):
                if o != e and c.cnt[o] > 0:
                    c._wait(e, ("e", o, c.cnt[o]))
            for s in range(NDS):
                if c.dval[s] > 0:
                    c._wait(e, ("d", s, c.dval[s]))
            for i in range(len(getattr(c, "csem", []))):
                c._wait(e, ("c", i, 1))

    def ar_src(self, li):
        nar = self.cfg.T // 1024

        def fn(t):
            b = self.arout[li * nar + t // 8]
            return b[(t % 8) * 128:(t % 8 + 1) * 128, :], [b]
        return fn

    def tm(self, name):
        if not hasattr(self, "tmarks"):
            self.tmarks = []
        self.tmarks.append((name, dict(self.c.cnt)))

    def mark(self):
        return len(self.c.guards)

    def release(self, m):
        self.barrier()
        c = self.c
        while len(c.guards) > m:
            c.guards.pop().__exit__(None, None, None)

    def ps(self):
        b = self.psf[self.psi % len(self.psf)]
        self.psi += 1
        return b

    def pst(self):
        return BfView(self.ps())

    def cst(self, n, bf=False, rows=128):
        o, w = self.cfg.cc[n]
        t = self.cb if bf else self.cf
        return t[0:rows, o:o + w]

    def load_w(self, dst, src_ap, ncols, eng_cycle=("dve", "pool")):
        c = self.c
        step = 128
        i = 0
        for o in range(0, ncols, step):
            n = min(step, ncols - o)
            st = self.wst[self.wsti % 2]
            self.wsti += 1
            c.dma("sp", st[:, :, 0:n], src_ap[:, o:o + n].rearrange("(c p) n -> p c n", p=128), w=[st], r=[self.w_in_d, self.w_kv_d])
            e = eng_cycle[i % len(eng_cycle)]
            i += 1
            c.op(e, lambda E: E.tensor_copy(dst[:, :, o:o + n], st[:, :, 0:n]), w=[dst], r=[st])

    def proj_fm(self, ps, wb, c0, m, tok0, ntok, rows0=0):
        c = self.c
        for k in range(8):
            c.op("pe", lambda E: E.matmul(ps[0:m, 0:ntok], wb[:, k, c0:c0 + m], self.hT[:, k, tok0:tok0 + ntok], start=(k == 0), stop=(k == 7)),
                 w=[ps], r=[wb, self.hT])

    def proj_tm(self, ps, wb, c0, n, tile, col0=0):
        c = self.c
        for k in range(8):
            c.op("pe", lambda E: E.matmul(ps[:, col0:col0 + n], self.hT[:, k, tile * 128:(tile + 1) * 128], wb[:, k, c0:c0 + n], start=(k == 0), stop=(k == 7)),
                 w=[ps], r=[wb, self.hT])

    def load_grow(self, key):
        o, w = self.cfg.rp[key]
        self.c.dma("sp", self.grow[:], self.rp_d[0:1, o:o + w].partition_broadcast(128), w=[self.grow], r=[self.rp_d])

    def rmsnorm_T(self, src_ap_fn, ntiles, gkey, dstT, src_bufs, add_aps=None):
        c = self.c
        self.load_grow(gkey)
        grow = self.grow[:]
        m_n = self.mark()
        xts = list(self.xt) + ([c.sb(f"xtx{i}", [128, D]) for i in range(2)] if ntiles > 4 else [])
        nx = len(xts)

        def s0(t):
            xb = xts[t % nx]
            c.dma("sp", xb[:], src_ap_fn(t), w=[xb], r=src_bufs)
            if add_aps is not None:
                for ai_, fn in enumerate(add_aps):
                    xa = self.xa[(t + ai_) % 2]
                    ap_, bufs = fn(t)
                    c.dma("sp", xa[:], ap_, w=[xa], r=bufs)
                    c.op("pool", lambda E: E.tensor_tensor(xb[:], xb[:], xa[:], ALU.add), w=[xb], r=[xb, xa])

        def s1(t):
            xb, hb, ss, rstd = xts[t % nx], self.hbs[t % 2], self.sss[t % 2], self.rstds[t % 2]
            c.op("act", lambda E: E.activation(hb[:], xb[:], AF.Square, accum_out=ss[:]), w=[hb, ss], r=[xb])
            c.op("pool", lambda E: E.tensor_scalar(rstd[:], ss[:], 1.0 / D, EPS, ALU.mult, ALU.add), w=[rstd], r=[ss])
            c.op("pool", lambda E: E.tensor_tensor(rstd[:], rstd[:], self.negh[:, 0:1], ALU.pow), w=[rstd], r=[rstd, self.negh])
            c.op("dve", lambda E: E.scalar_tensor_tensor(hb[:], xb[:], rstd[:], grow, ALU.mult, ALU.mult), w=[hb], r=[xb, rstd, self.grow])

        def s2(t):
            hb = self.hbs[t % 2]
            pT = self.pst()
            for k in range(8):
                c.op("pe", lambda E: E.transpose(pT[:, k * 128:(k + 1) * 128], hb[:, k * 128:(k + 1) * 128], self.cst("ident", True)), w=[pT], r=[hb, self.cb])
            c.op("act", lambda E: E.copy(dstT[:, :, t * 128:(t + 1) * 128], pT[:].rearrange("p (k n) -> p k n", k=8)), w=[dstT], r=[pT])

        pipeline(ntiles, [s0, s1, s2])
        if nx > 2:
            self.release(m_n)

    def build(self):
        cfg, c, nc = self.cfg, self.c, self.nc
        T, NT, NB = cfg.T, cfg.NT, cfg.NB
        L = cfg.L
        self.x_d = c.dram("x", [T, D], F32, "ExternalInput")
        self.mem_d = c.dram("mem", [cfg.MEM, D], F32, "ExternalInput")
        self.pos_d = c.dram("pos", [1, T], I32, "ExternalInput")
        self.w_in_d = c.dram("w_in", [L, D, cfg.NIN], F32, "ExternalInput")
        self.w_kv_d = c.dram("w_kv", [L, D, 2 * cfg.HD * 128], F32, "ExternalInput")
        self.w_out_d = c.dram("w_out", [L, cfg.NY, D], F32, "ExternalInput")
        self.pp_d = c.dram("pp", [128, max(cfg.NPP, 1)], F32, "ExternalInput")
        self.rp_d = c.dram("rp", [1, cfg.NRP], F32, "ExternalInput")
        self.cc_d = c.dram("cc", [128, cfg.NCC], F32, "ExternalInput")
        if self.mode == "add2":
            self.pa_d = c.dram("pa", [T, D], F32, "ExternalInput")
            self.pb_d = c.dram("pb", [T, D], F32, "ExternalInput")
        self.out_d = c.dram("out", [T, D], F32, "ExternalOutput")
        self.yT_d = c.dram("yT_s", [cfg.NY, T], BF16, "ExternalOutput" if getattr(self, "dbg_y", False) else "Internal")
        self.x1_d = c.dram("x1_s", [T, D], F32)
        self.hf_d = c.dram("hf_s", [T, 256], F32)
        self.hb_d = c.dram("hb_s", [T, 256], F32)
        if self.debug_out:
            self.dbg_d = c.dram("dbg", list(self.debug_out), F32, "ExternalOutput")
        if self.mode == "split":
            self.groups = [[0, 1], [2, 3], [4, 5], [6, 7]]
            nar = T // 1024
            self.arin = [Buf(nc.dram_tensor(f"arin{i}", [1024, D], F32), f"arin{i}") for i in range(nar * len(self.layers))]
            self.arout = [Buf(nc.dram_tensor(f"arout{i}", [1024, D], F32), f"arout{i}") for i in range(nar * len(self.layers))]

        self.cf = c.sb("cf", [128, cfg.NCC])
        self.cb = c.sb("cb", [128, cfg.NCC], BF16)
        self.ppt = c.sb("ppt", [128, max(cfg.NPP, 1)])
        self.rowt = c.sb("rowt", [128, cfg.NRS])
        self.grow = c.sb("grow", [128, D])
        self.hT = c.sb("hT", [128, 8, T], BF16)
        self.wst = [c.sb(f"wst{i}", [128, 8, 128]) for i in range(2)]
        self.wsti = 0
        self.xt = [c.sb(f"xt{i}", [128, D]) for i in range(2)]
        self.xa = [c.sb(f"xa{i}", [128, D]) for i in range(1)] * 2 if self.mode in ("add2", "split") else None
        self.sss = [c.sb(f"ss{i}", [128, 1]) for i in range(2)]
        self.rstds = [c.sb(f"rstd{i}", [128, 1]) for i in range(2)]
        self.ss, self.rstd = self.sss[0], self.rstds[0]
        self.epsb = c.sb("epsb", [128, 1])
        self.hbs = [c.sb(f"hb{i}", [128, D], BF16) for i in range(2)]
        self.psf = [c.ps(f"psf{i}", [128, 512]) for i in range(8)]
        self.psi = 0

        c.dma("sp", self.cf[:], self.cc_d[:], w=[self.cf], r=[self.cc_d])
        c.dma("sp", self.ppt[:], self.pp_d[:], w=[self.ppt], r=[self.pp_d])
        c.dma("sp", self.rowt[:], self.rp_d[0:1, 0:cfg.NRS].partition_broadcast(128), w=[self.rowt], r=[self.rp_d])
        c.op("dve", lambda E: E.tensor_copy(self.cb[:], self.cf[:]), w=[self.cb], r=[self.cf])
        c.op("pool", lambda E: E.memset(self.epsb[:], EPS), w=[self.epsb])
        self.negh = c.sb("negh", [128, 4])
        c.op("pool", lambda E: E.memset(self.negh[:], -0.5), w=[self.negh])

        if "B" in self.phases:
            self.rope_tables()

        for l in self.layers:
            last = (l == L - 1)
            if l == self.layers[0]:
                src, srcb = (lambda t: self.x_d[t * 128:(t + 1) * 128, :]), [self.x_d]
                adds = None
                if self.mode == "add2":
                    adds = [(lambda t: (self.pa_d[t * 128:(t + 1) * 128, :], [self.pa_d])),
                            (lambda t: (self.pb_d[t * 128:(t + 1) * 128, :], [self.pb_d]))]
            elif self.mode == "split":
                src, srcb = (lambda t: self.x_d[t * 128:(t + 1) * 128, :]), [self.x_d]
                adds = [self.ar_src(li) for li in range(self.layers.index(l))]
            else:
                src, srcb = (lambda t: self.x1_d[t * 128:(t + 1) * 128, :]), [self.x1_d]
                adds = None
            self.xsrc = (src, srcb, adds)
            if "N" in self.phases:
                self.rmsnorm_T(src, NT, ("ng", l), self.hT, srcb, adds)
            if "D" in self.phases:
                m = self.mark()
                self.phase_D(l)
                self.release(m)
            if "B" in self.phases:
                m = self.mark()
                self.phase_B(l)
                self.release(m)
            if "A" in self.phases:
                m = self.mark()
                self.phase_A(l)
                self.release(m)
            if "C" in self.phases:
                m = self.mark()
                self.phase_C(l)
                self.release(m)
            if "O" in self.phases:
                m = self.mark()
                self.phase_O(l, last)
                self.release(m)
        outs = [self.out_d] + ([self.dbg_d] if self.debug_out else [])
        c.finish(outs, "sp")
        c.finish(outs, "pool")
        self.barrier()
        c.close()
        return nc

    def phase_D(self, l):
        cfg, c = self.cfg, self.c
        T, NB, HD = cfg.T, cfg.NB, cfg.HD
        M = cfg.MEM
        memT = c.sb("memT", [128, 8, M], BF16)
        wkv = c.sb("wkv", [128, 8, 2 * HD * 128], BF16)
        KT = c.sb("KT", [128, HD, M], BF16)
        Vt = c.sb("Vt", [128, M // 128, HD * 128], BF16)
        wq = c.sb("wdq", [128, 8, HD * 128], BF16)
        wz = c.sb("wdz", [128, 8, HD * 128], BF16)
        qTb = [c.sb(f"dq{i}", [128, 512], BF16) for i in range(2)]
        szb = [c.sb(f"dsz{i}", [128, 512], BF16) for i in range(2)]
        Eb = [c.sb(f"dE{i}", [128, 512], BF16) for i in range(4)]
        rden = [c.sb(f"drd{i}", [128, 512]) for i in range(2)]
        y1 = [c.sb(f"dy1{i}", [128, 512]) for i in range(2)]
        yb = [c.sb(f"dyb{i}", [128, 512], BF16) for i in range(2)]
        self.rmsnorm_T(lambda t: self.mem_d[t * 128:(t + 1) * 128, :], M // 128, ("mg", l), memT, [self.mem_d])
        self.load_w(wkv, self.w_kv_d[l], 2 * HD * 128)
        qo, _ = cfg.col["D_q"]
        zo, _ = cfg.col["D_z"]
        self.load_w(wq, self.w_in_d[l][:, qo:qo + HD * 128], HD * 128)
        self.load_w(wz, self.w_in_d[l][:, zo:zo + HD * 128], HD * 128)
        for h in range(HD):
            ps = self.ps()
            for k in range(8):
                c.op("pe", lambda E: E.matmul(ps[:, 0:M], wkv[:, k, h * 128:(h + 1) * 128], memT[:, k, :], start=(k == 0), stop=(k == 7)), w=[ps], r=[wkv, memT])
            c.op("act", lambda E: E.copy(KT[:, h, :], ps[:, 0:M]), w=[KT], r=[ps])
        for j in range(M // 128):
            ps = self.ps()
            for k in range(8):
                c.op("pe", lambda E: E.matmul(ps[:, 0:HD * 128], memT[:, k, j * 128:(j + 1) * 128], wkv[:, k, HD * 128:2 * HD * 128], start=(k == 0), stop=(k == 7)), w=[ps], r=[wkv, memT])
            c.op("act", lambda E: E.copy(Vt[:, j, :], ps[:, 0:HD * 128]), w=[Vt], r=[ps])
        scale = 128.0 ** -0.5
        nj = M // 128
        units = [(nb, h) for nb in range(NB) for h in range(HD)]
        Eb6 = Eb + [c.sb(f"dE{i}", [128, 512], BF16) for i in range(4, 4 + 2 * nj - 4 + 4)] if False else Eb
        state = {}

        def s0(u):
            nb, h = units[u]
            q = qTb[u % 2]
            ps = self.ps()
            self.proj_fm(ps, wq, h * 128, 128, nb * 512, 512)
            c.op("act", lambda E: E.copy(q[:], ps[:]), w=[q], r=[ps])

        def s1(u):
            nb, h = units[u]
            q = qTb[u % 2]
            Es = []
            for j in range(nj):
                ps = self.ps()
                c.op("pe", lambda E: E.matmul(ps[:], KT[:, h, j * 128:(j + 1) * 128], q[:], start=True, stop=True), w=[ps], r=[KT, q])
                Ej = Eb[(u * nj + j) % 4]
                c.op("act", lambda E: E.activation(Ej[:], ps[:], AF.Exp, scale=scale), w=[Ej], r=[ps])
                Es.append(Ej)
            state[u] = Es

        def s2(u):
            nb, h = units[u]
            sz, rd, yy, ybb = szb[u % 2], rden[u % 2], y1[u % 2], yb[u % 2]
            Es = state.pop(u)
            ps = self.ps()
            self.proj_fm(ps, wz, h * 128, 128, nb * 512, 512)
            c.op("act", lambda E: E.activation(sz[:], ps[:], AF.Silu), w=[sz], r=[ps])
            po = self.ps()
            pd = self.ps()
            for j in range(nj):
                c.op("pe", lambda E: E.matmul(po[:], Vt[:, j, h * 128:(h + 1) * 128], Es[j][:], start=(j == 0), stop=(j == nj - 1)), w=[po], r=[Vt, Es[j]])
            for j in range(nj):
                c.op("pe", lambda E: E.matmul(pd[:], self.cst("ones", True), Es[j][:], start=(j == 0), stop=(j == nj - 1)), w=[pd], r=[self.cb, Es[j]])
            c.op("dve", lambda E: E.reciprocal(rd[:], pd[:]), w=[rd], r=[pd])
            c.op("dve", lambda E: E.tensor_tensor(yy[:], po[:], rd[:], ALU.mult), w=[yy], r=[po, rd])
            c.op("pool", lambda E: E.tensor_tensor(ybb[:], yy[:], sz[:], ALU.mult), w=[ybb], r=[yy, sz])
            r0 = cfg.yD + h * 128
            c.dma("sp", self.yT_d[r0:r0 + 128, nb * 512:(nb + 1) * 512], ybb[:], w=[self.yT_d], r=[ybb])

        pipeline(len(units), [s0, s1, s2])

    def phase_O(self, l, last):
        cfg, c = self.cfg, self.c
        T, NT, NY = cfg.T, cfg.NT, cfg.NY
        chunks = []
        for r in range(0, cfg.yB, 128):
            chunks.append((r, 128))
        for r in range(cfg.yB, cfg.yC, 64):
            chunks.append((r, 64))
        for r in range(cfg.yC, NY, 128):
            chunks.append((r, 128))
        nch = len(chunks)
        wo = c.sb("wo", [128, nch, D], BF16)
        wos = [c.sb(f"wos{i}", [128, D]) for i in range(2)]
        for i, (r, n) in enumerate(chunks):
            st = wos[i % 2]
            c.dma("sp", st[0:n, :], self.w_out_d[l][r:r + n, :], w=[st], r=[self.w_out_d])
            c.op("pool" if i % 2 else "dve", lambda E: E.tensor_copy(wo[0:n, i, :], st[0:n, :]), w=[wo], r=[st])
        yt = [c.sb(f"oyt{i}", [128, nch, 128], BF16) for i in range(2)]
        xo = [c.sb(f"oxo{i}", [128, D]) for i in range(2)]
        src, srcb, adds = self.xsrc
        if last and self.mode != "partial":
            self.load_grow(("fg", 0))
        nA = cfg.yB // 128
        nBc = (cfg.yC - cfg.yB) // 64
        nCD = (NY - cfg.yC) // 128
        nar = T // 1024
        li = self.layers.index(l)

        def o_load(t):
            y = yt[t % 2]
            if nA:
                c.dma("sp", y[:, 0:nA, :], self.yT_d[0:cfg.yB, t * 128:(t + 1) * 128].rearrange("(c p) n -> p c n", p=128), w=[y], r=[self.yT_d])
            if nBc:
                c.dma("sp", y[0:64, nA:nA + nBc, :], self.yT_d[cfg.yB:cfg.yC, t * 128:(t + 1) * 128].rearrange("(c p) n -> p c n", p=64), w=[y], r=[self.yT_d])
            if nCD:
                c.dma("sp", y[:, nA + nBc:nch, :], self.yT_d[cfg.yC:NY, t * 128:(t + 1) * 128].rearrange("(c p) n -> p c n", p=128), w=[y], r=[self.yT_d])
            if self.mode not in ("partial", "split"):
                xb = self.xt[t % 2]
                c.dma("sp", xb[:], src(t), w=[xb], r=srcb)
                if adds is not None:
                    for ai_, fn in enumerate(adds):
                        xa = self.xa[(t + ai_) % 2]
                        ap_, bufs = fn(t)
                        c.dma("sp", xa[:], ap_, w=[xa], r=bufs)
                        c.op("pool", lambda E: E.tensor_tensor(xb[:], xb[:], xa[:], ALU.add), w=[xb], r=[xb, xa])

        def o_comp(t):
            y, xb, xn = yt[t % 2], self.xt[t % 2], xo[t % 2]
            for half in range(2):
                ps = self.ps()
                for i, (r, n) in enumerate(chunks):
                    c.op("pe", lambda E: E.matmul(ps[:], y[0:n, i, :], wo[0:n, i, half * 512:(half + 1) * 512], start=(i == 0), stop=(i == nch - 1)), w=[ps], r=[y, wo])
                if self.mode in ("partial", "split"):
                    c.op("act", lambda E: E.copy(xn[:, half * 512:(half + 1) * 512], ps[:]), w=[xn], r=[ps])
                else:
                    c.op("dve", lambda E: E.tensor_tensor(xn[:, half * 512:(half + 1) * 512], ps[:], xb[:, half * 512:(half + 1) * 512], ALU.add), w=[xn], r=[ps, xb])
            if last and self.mode not in ("partial", "split"):
                self.final_norm(xn)

        def o_store(t):
            xn = xo[t % 2]
            if self.mode == "split":
                ch = li * nar + t // 8
                c.dma("sp", self.arin[ch][(t % 8) * 128:(t % 8 + 1) * 128, :], xn[:], w=[self.arin[ch]], r=[xn])
                if t % 8 == 7:
                    c.collective("AllReduce", ALU.add, self.groups, self.arin[ch], self.arout[ch])
            elif self.mode == "partial" or last:
                c.dma("sp", self.out_d[t * 128:(t + 1) * 128, :], xn[:], w=[self.out_d], r=[xn])
            else:
                c.dma("sp", self.x1_d[t * 128:(t + 1) * 128, :], xn[:], w=[self.x1_d], r=[xn])

        pipeline(NT, [o_load, o_comp, o_store])

        if self.mode == "split" and last:
            fns = list(adds or []) + [self.ar_src(li)]

            def f_load(t):
                xn = xo[t % 3]
                c.dma("sp", xn[:], src(t), w=[xn], r=srcb)
                for ai_, fn in enumerate(fns):
                    xa = self.xa3[(t * len(fns) + ai_) % 3]
                    ap_, bufs = fn(t)
                    c.dma("sp", xa[:], ap_, w=[xa], r=bufs)
                    c.op("pool", lambda E: E.tensor_tensor(xn[:], xn[:], xa[:], ALU.add), w=[xn], r=[xn, xa])

            def f_comp(t):
                self.final_norm(xo[t % 3])

            def f_store(t):
                xn = xo[t % 3]
                c.dma("sp", self.out_d[t * 128:(t + 1) * 128, :], xn[:], w=[self.out_d], r=[xn])

            self.xa3 = list(self.xa[0:1]) + [self.xt[0], self.xt[1]]
            xo.append(c.sb("oxo2", [128, D]))
            pipeline(NT, [f_load, f_comp, f_store])

    def final_norm(self, xn):
        c = self.c
        c.op("act", lambda E: E.activation(self.hbs[0][:], xn[:], AF.Square, accum_out=self.ss[:]), w=[self.hbs[0], self.ss], r=[xn])
        c.op("pool", lambda E: E.tensor_scalar(self.rstd[:], self.ss[:], 1.0 / D, EPS, ALU.mult, ALU.add), w=[self.rstd], r=[self.ss])
        c.op("pool", lambda E: E.tensor_tensor(self.rstd[:], self.rstd[:], self.negh[:, 0:1], ALU.pow), w=[self.rstd], r=[self.rstd, self.negh])
        c.op("dve", lambda E: E.scalar_tensor_tensor(xn[:], xn[:], self.rstd[:], self.grow[:], ALU.mult, ALU.mult), w=[xn], r=[xn, self.rstd, self.grow])


SIZES = (512, 512, 512, 512, 512, 512, 128, 128, 512, 256, 256, 512, 512, 512, 8, 8, 512, 512)
NAMES = ("a_q", "a_i", "a_ff", "a_fb", "a_z", "b_q", "b_k", "b_v", "b_z", "c_q", "c_k", "c_v", "c_o", "c_z", "c_ig", "c_fg", "d_q", "d_z")
OFFS = dict(zip(NAMES, np.concatenate([[0], np.cumsum(SIZES)[:-1]]).tolist()))


def pack_weights(cfg, inp, selA, selB, selC, selD):
    L = cfg.L
    w_in = inp["w_in"]
    cols = []
    for n in ("a_q", "a_i", "a_ff", "a_fb", "a_z"):
        cols += [OFFS[n] + h * 128 + np.arange(128) for h in selA]
    cols += [OFFS["b_q"] + (g * 4 + j) * 64 + np.arange(64) for g in selB for j in range(4)]
    cols += [OFFS["b_k"] + g * 64 + np.arange(64) for g in selB]
    cols += [OFFS["b_v"] + g * 64 + np.arange(64) for g in selB]
    cols += [OFFS["b_z"] + (g * 4 + j) * 64 + np.arange(64) for g in selB for j in range(4)]
    cols += [OFFS["c_q"] + h * 64 + np.arange(64) for h in selC]
    cols += [OFFS["c_k"] + h * 64 + np.arange(64) for h in selC]
    for n in ("c_v", "c_o", "c_z"):
        cols += [OFFS[n] + h * 128 + np.arange(128) for h in selC]
    for p in range(len(selC) // 2):
        hh = selC[2 * p:2 * p + 2]
        cols += [np.array([OFFS["c_ig"] + h for h in hh] + [OFFS["c_ig"] + 4 + h for h in hh]
                          + [OFFS["c_fg"] + h for h in hh] + [OFFS["c_fg"] + 4 + h for h in hh])]
    for n in ("d_q", "d_z"):
        cols += [OFFS[n] + h * 128 + np.arange(128) for h in selD]
    cols = np.concatenate(cols)
    assert len(cols) == cfg.NIN, (len(cols), cfg.NIN)
    w_in_c = np.ascontiguousarray(w_in[:, :, cols])
    kvc = np.concatenate([h * 128 + np.arange(128) for h in selD] + [512 + h * 128 + np.arange(128) for h in selD])
    w_kv_c = np.ascontiguousarray(inp["w_mem_kv"][:, :, kvc])
    rows = np.concatenate([h * 128 + np.arange(128) for h in selA] + [512 + (g * 4 + j) * 64 + np.arange(64) for g in selB for j in range(4)]
                          + [1024 + h * 128 + np.arange(128) for h in selC] + [1536 + h * 128 + np.arange(128) for h in selD])
    w_out_c = np.ascontiguousarray(inp["w_out"][:, rows, :])
    pp = np.zeros((128, max(cfg.NPP, 1)), np.float32)
    for l in range(L):
        for i, h in enumerate(selA):
            o = cfg.pp[("A", l, i)]
            pp[:, o] = inp["hgrn_lb_logits"][0, h * 128:(h + 1) * 128]
            pp[:, o + 1] = inp["hgrn_lb_logits"][l, h * 128:(h + 1) * 128]
            pp[:, o + 2] = inp["hgrn_norm_g"][l, h * 128:(h + 1) * 128]
        for p in range(len(selC) // 2):
            o = cfg.pp[("Cw", l, p)]
            ch = np.concatenate([selC[2 * p] * 64 + np.arange(64), selC[2 * p + 1] * 64 + np.arange(64)])
            for j in range(5):
                pp[:, o + j] = inp["mlstm_conv_w"][l, j, ch]
                pp[:, o + 5 + j] = inp["mlstm_conv_w"][l, j, 256 + ch]
    rp = np.zeros((1, cfg.NRP), np.float32)
    for l in range(L):
        o, w = cfg.rp[("ng", l)]
        rp[0, o:o + w] = inp["norm_g"][l]
        o, w = cfg.rp[("mg", l)]
        rp[0, o:o + w] = inp["mem_norm_g"][l]
        o, w = cfg.rp[("gC", l)]
        rp[0, o:o + w] = np.concatenate([inp["mlstm_norm_g"][l, h * 128:(h + 1) * 128] for h in selC])
        o, w = cfg.rp[("gb", l)]
        gb = inp["mlstm_gate_b"][l]
        rp[0, o:o + w] = np.concatenate([[gb[h] for h in selC[2 * p:2 * p + 2]] + [gb[4 + h] for h in selC[2 * p:2 * p + 2]]
                                         + [gb[8 + h] for h in selC[2 * p:2 * p + 2]] + [gb[12 + h] for h in selC[2 * p:2 * p + 2]]
                                         for p in range(len(selC) // 2)])
        o, w = cfg.rp[("sink", l)]
        rp[0, o:o + w] = np.concatenate([inp["attn_sink"][l, g * 4:(g + 1) * 4] for g in selB])
    o, w = cfg.rp[("fg", 0)]
    rp[0, o:o + w] = inp["final_norm_g"]
    return {"w_in": w_in_c, "w_kv": w_kv_c, "w_out": w_out_c, "pp": pp, "rp": rp, "cc": make_consts(cfg)}


def _bc(ap, axis, n):
    a = ap.unsqueeze(axis)
    shp = list(a.shape)
    shp[axis] = n
    return a.broadcast_to(shp)


def rope_tables(self):
    cfg, c = self.cfg, self.c
    T = cfg.T
    self.cs_d = c.dram("cs_s", [2, 64, T], F32)
    m = self.mark()
    CH = min(T, 1024)
    posi = c.sb("posi", [16, CH], I32)
    tt = c.sb("rtt", [16, CH])
    kf = c.sb("rkf", [16, CH])
    ki = c.sb("rki", [16, CH], I32)
    mm = c.sb("rmm", [16, CH])
    C64 = c.sb("rC64", [64, CH])
    S64 = c.sb("rS64", [64, CH])
    sc = 2 * math.pi * (1 - 2e-6)
    for c0 in range(0, T, CH):
        c.dma("sp", posi[:], self.pos_d[0:1, c0:c0 + CH].partition_broadcast(16), w=[posi], r=[self.pos_d])
        c.op("dve", lambda E: E.tensor_copy(tt[:], posi[:]), w=[tt], r=[posi])
        c.op("dve", lambda E: E.tensor_scalar(tt[:], tt[:], self.cst("invf", rows=16), None, ALU.mult), w=[tt], r=[tt, self.cf])
        c.op("dve", lambda E: E.tensor_copy(ki[:], tt[:]), w=[ki], r=[tt])
        c.op("dve", lambda E: E.tensor_copy(kf[:], ki[:]), w=[kf], r=[ki])
        c.op("dve", lambda E: E.tensor_tensor(tt[:], tt[:], kf[:], ALU.subtract), w=[tt], r=[tt, kf])
        c.op("pool", lambda E: E.memset(C64[:], 1.0), w=[C64])
        c.op("pool", lambda E: E.memset(S64[:], 0.0), w=[S64])
        c.op("dve", lambda E: E.tensor_single_scalar(mm[:], tt[:], 0.5, ALU.is_gt), w=[mm], r=[tt])
        c.op("dve", lambda E: E.tensor_tensor(kf[:], tt[:], mm[:], ALU.subtract), w=[kf], r=[tt, mm])
        c.op("act", lambda E: E.activation(S64[0:16, :], kf[:], AF.Sin, scale=sc), w=[S64], r=[kf])
        c.op("dve", lambda E: E.tensor_scalar_add(tt[:], tt[:], 0.25), w=[tt], r=[tt])
        c.op("dve", lambda E: E.tensor_single_scalar(mm[:], tt[:], 0.5, ALU.is_gt), w=[mm], r=[tt])
        c.op("dve", lambda E: E.tensor_tensor(tt[:], tt[:], mm[:], ALU.subtract), w=[tt], r=[tt, mm])
        c.op("dve", lambda E: E.tensor_single_scalar(mm[:], tt[:], 0.5, ALU.is_gt), w=[mm], r=[tt])
        c.op("dve", lambda E: E.tensor_tensor(kf[:], tt[:], mm[:], ALU.subtract), w=[kf], r=[tt, mm])
        c.op("act", lambda E: E.activation(C64[0:16, :], kf[:], AF.Sin, scale=sc), w=[C64], r=[kf])
        c.dma("sp", self.cs_d[0][:, c0:c0 + CH], C64[:], w=[self.cs_d], r=[C64])
        c.dma("sp", self.cs_d[1][:, c0:c0 + CH], S64[:], w=[self.cs_d], r=[S64])
    self.release(m)


def phase_B(self, l):
    cfg, c = self.cfg, self.c
    T, NT, NB, GB = cfg.T, cfg.NT, cfg.NB, cfg.GB
    wq = c.sb("wbq", [128, 8, GB * 256], BF16)
    wk = c.sb("wbk", [128, 8, GB * 64], BF16)
    wv = c.sb("wbv", [128, 8, GB * 64], BF16)
    wz = c.sb("wbz", [128, 8, GB * 256], BF16)
    for (wb, n) in ((wq, "B_q"), (wk, "B_k"), (wv, "B_v"), (wz, "B_z")):
        o, w = cfg.col[n]
        self.load_w(wb, self.w_in_d[l][:, o:o + w], w)
    kT = c.sb("bkT", [64, GB, T], BF16)
    v_tm = c.sb("bvtm", [128, NT, GB * 64], BF16)
    sinkexp = c.sb("bsink", [64, GB * 4])
    cs = [c.sb(f"bcs{i}", [64, 2, 512]) for i in range(2)]
    xf = [c.sb(f"bxf{i}", [64, 512]) for i in range(2)]
    t1 = [c.sb(f"bt1{i}", [64, 512]) for i in range(2)]
    t2 = [c.sb(f"bt2{i}", [64, 512]) for i in range(2)]
    qT = [c.sb(f"bqT{i}", [64, 4, 512], BF16) for i in range(2)]
    szB = [c.sb(f"bsz{i}", [64, 4, 512], BF16) for i in range(2)]
    Eb = [c.sb(f"bE{i}", [128, 4, 128], BF16) for i in range(6)]
    den = [c.sb(f"bden{i}", [64, 4, 128]) for i in range(2)]
    y1 = [c.sb(f"by1{i}", [64, 4, 128]) for i in range(2)]
    yb = [c.sb(f"byb{i}", [64, 4, 128], BF16) for i in range(2)]
    so, sw = cfg.rp[("sink", l)]
    c.op("act", lambda E: E.activation(sinkexp[:], self.rowt[0:64, so:so + sw], AF.Exp), w=[sinkexp], r=[self.rowt])
    P64 = self.cf[0:64, cfg.cc["P64T"][0]:cfg.cc["P64T"][0] + 64]
    self._ri = 0

    def rope(ps, cst, dst_ap, dst_buf):
        i = self._ri
        self._ri += 1
        x, a, b = xf[i % 2], t1[i % 2], t2[i % 2]
        c.op("act", lambda E: E.copy(x[:], ps[0:64, :]), w=[x], r=[ps])
        pr = self.ps()
        c.op("pe", lambda E: E.matmul(pr[0:64, :], P64, x[:], start=True, stop=True), w=[pr], r=[self.cf, x])
        c.op("pool", lambda E: E.tensor_tensor(a[:], x[:], cst[:, 0, :], ALU.mult), w=[a], r=[x, cst])
        c.op("dve", lambda E: E.tensor_tensor(b[:], pr[0:64, :], cst[:, 1, :], ALU.mult), w=[b], r=[pr, cst])
        c.op("pool", lambda E: E.tensor_tensor(dst_ap, a[:], b[:], ALU.add), w=[dst_buf], r=[a, b])

    def load_cs(nb):
        t = cs[nb % 2]
        c.dma("sp", t[:], self.cs_d[:, :, nb * 512:(nb + 1) * 512].rearrange("a p n -> p a n"), w=[t], r=[self.cs_d])
        return t

    for nb in range(NB):
        cst = load_cs(nb)
        for g in range(GB):
            ps = self.ps()
            self.proj_fm(ps, wk, g * 64, 64, nb * 512, 512)
            rope(ps, cst, kT[:, g, nb * 512:(nb + 1) * 512], kT)
        for tt in range(4):
            tile = nb * 4 + tt
            ps = self.ps()
            self.proj_tm(ps, wv, 0, GB * 64, tile)
            c.op("act", lambda E: E.copy(v_tm[:, tile, :], ps[:, 0:GB * 64]), w=[v_tm], r=[ps])
    self.tm("B.pass1_done")
    onesb = self.cb[:, cfg.cc["ones"][0]:cfg.cc["ones"][0] + 64]
    mL = self.cb[:, cfg.cc["triL"][0]:cfg.cc["triL"][0] + 128]
    mU = self.cb[:, cfg.cc["triU"][0]:cfg.cc["triU"][0] + 128]
    Eb9 = Eb + [c.sb(f"bE{i}", [128, 4, 128], BF16) for i in range(6, 9)]
    units = [(nb, g, qt) for nb in range(NB) for g in range(GB) for qt in range(4)]
    state = {}

    def proj(nb, g):
        bi = (nb * GB + g) % 2
        q, sz = qT[bi], szB[bi]
        cst = load_cs(nb)
        for j in range(4):
            ps = self.ps()
            self.proj_fm(ps, wq, (g * 4 + j) * 64, 64, nb * 512, 512)
            rope(ps, cst, q[:, j, :], q)
            ps = self.ps()
            self.proj_fm(ps, wz, (g * 4 + j) * 64, 64, nb * 512, 512)
            c.op("act", lambda E: E.activation(sz[:, j, :], ps[0:64, :], AF.Silu), w=[sz], r=[ps])

    def sA(u):
        nb, g, qt = units[u]
        q = qT[(nb * GB + g) % 2]
        Tq = nb * 4 + qt
        kts = [k for k in (Tq - 1, Tq, Tq + 1) if 0 <= k < NT]
        Es = []
        for i, kt in enumerate(kts):
            ps = self.ps()
            c.op("pe", lambda E: E.matmul(ps[:].rearrange("p (j n) -> p j n", j=4), kT[:, g, kt * 128:(kt + 1) * 128], q[:, :, qt * 128:(qt + 1) * 128], start=True, stop=True), w=[ps], r=[kT, q])
            Ek = Eb9[(u * 3 + i) % 9]
            c.op("act", lambda E: E.activation(Ek[:], ps[:].rearrange("p (j n) -> p j n", j=4), AF.Exp, scale=0.125), w=[Ek], r=[ps])
            if kt != Tq:
                mk = mL if kt < Tq else mU
                c.op("pool", lambda E: E.tensor_tensor(Ek[:], Ek[:], _bc(mk, 1, 4), ALU.mult), w=[Ek], r=[Ek, self.cb])
            Es.append(Ek)
        state[u] = (kts, Es)

    def sB(u):
        nb, g, qt = units[u]
        sz = szB[(nb * GB + g) % 2]
        Tq = nb * 4 + qt
        kts, Es = state.pop(u)
        po = self.ps()
        pd = self.ps()
        n = len(kts)
        for i, kt in enumerate(kts):
            c.op("pe", lambda E: E.matmul(po[0:64, :], v_tm[:, kt, g * 64:(g + 1) * 64], Es[i][:].rearrange("p j n -> p (j n)"), start=(i == 0), stop=(i == n - 1)), w=[po], r=[v_tm, Es[i]])
        for i, kt in enumerate(kts):
            c.op("pe", lambda E: E.matmul(pd[0:64, :], onesb, Es[i][:].rearrange("p j n -> p (j n)"), start=(i == 0), stop=(i == n - 1)), w=[pd], r=[self.cb, Es[i]])
        dn, yy, ybb = den[u % 2], y1[u % 2], yb[u % 2]
        c.op("dve", lambda E: E.tensor_tensor(dn[:], pd[0:64, :].rearrange("p (j n) -> p j n", j=4), _bc(sinkexp[:, g * 4:(g + 1) * 4], 2, 128), ALU.add), w=[dn], r=[pd, sinkexp])
        c.op("dve", lambda E: E.reciprocal(dn[:], dn[:]), w=[dn], r=[dn])
        c.op("dve", lambda E: E.tensor_tensor(yy[:], po[0:64, :].rearrange("p (j n) -> p j n", j=4), dn[:], ALU.mult), w=[yy], r=[po, dn])
        c.op("pool", lambda E: E.tensor_tensor(ybb[:], yy[:], sz[:, :, qt * 128:(qt + 1) * 128], ALU.mult), w=[ybb], r=[yy, sz])
        r0 = cfg.yB + g * 256
        c.dma("sp", self.yT_d[r0:r0 + 256, Tq * 128:(Tq + 1) * 128].rearrange("(j p) n -> p j n", p=64), ybb[:], w=[self.yT_d], r=[ybb])

    blocks = [(nb, g) for nb in range(NB) for g in range(GB)]
    proj(*blocks[0])
    if len(blocks) > 1:
        proj(*blocks[1])
    nu = len(units)
    for u in range(nu + 2):
        if 0 <= u - 2 < nu:
            sB(u - 2)
        if u % 4 == 2 and u // 4 + 1 < len(blocks) and u // 4 >= 1:
            proj(*blocks[u // 4 + 1])
        if u < nu:
            sA(u)


Gen.rope_tables = rope_tables
Gen.phase_B = phase_B


def phase_A(self, l):
    for h in range(self.cfg.HA):
        m = self.mark()
        self.phase_A_head(l, h)
        self.release(m)


def phase_A_head(self, l, h):
    cfg, c = self.cfg, self.c
    T, NT, NB = cfg.T, cfg.NT, cfg.NB
    NCH = T // 64
    qt = {d: c.sb(f"aqt{d}", [128, T], BF16) for d in "fb"}
    kt = {d: c.sb(f"akt{d}", [128, T], BF16) for d in "fb"}
    ktm = {d: c.sb(f"aktm{d}", [128, NT, 128], BF16) for d in "fb"}
    ebl = {d: c.sb(f"aebl{d}", [128, NCH]) for d in "fb"}
    v_tm = c.sb("avtm", [128, NT, 128], BF16)
    wz = c.sb("awz", [128, 8, 128], BF16)
    lbt = c.sb("albt", [128, 1])
    om = c.sb("aom", [128, 1])
    scanm = self.cst("scanm")
    ident = self.cst("ident", True)
    onesb = self.cst("ones", True)
    mask = {"f": self.cst("bdU"), "b": self.cst("bdL")}
    zo_, _ = cfg.col["A_z"]
    self.load_w_into(wz, 0, self.w_in_d[l][:, zo_ + h * 128:zo_ + (h + 1) * 128], 128)
    po_ = cfg.pp[("A", l, h)]
    if l == 0:
        c.op("pool", lambda E: E.memset(lbt[:], 0.0), w=[lbt])
        c.op("pool", lambda E: E.memset(om[:], 1.0), w=[om])
    else:
        c.op("dve", lambda E: E.tensor_tensor(lbt[:], self.ppt[:, po_ + 1:po_ + 2], self.ppt[:, po_:po_ + 1], ALU.subtract), w=[lbt], r=[self.ppt])
        c.op("act", lambda E: E.activation(lbt[:], lbt[:], AF.Sigmoid), w=[lbt], r=[lbt])
        c.op("dve", lambda E: E.tensor_scalar(om[:], lbt[:], -1.0, 1.0, ALU.mult, ALU.add), w=[om], r=[lbt])
    gA = self.ppt[:, po_ + 2:po_ + 3]
    mpre = self.mark()
    w4 = c.sb("aw4", [128, 8, 4 * 128], BF16)
    for j, n in enumerate(("A_q", "A_i", "A_ff", "A_fb")):
        o, _ = cfg.col[n]
        self.load_w_into(w4, j * 128, self.w_in_d[l][:, o + h * 128:o + (h + 1) * 128], 128)
    qf = [c.sb(f"aqf{i}", [128, 512]) for i in range(2)]
    tA = [c.sb(f"atA{i}", [128, 512]) for i in range(2)]
    tG = [c.sb(f"atG{i}", [128, 512]) for i in range(2)]
    tK = [c.sb(f"atK{i}", [128, 512]) for i in range(2)]
    tB = [c.sb(f"atB{i}", [128, 512]) for i in range(2)]
    tE1 = [c.sb(f"atE1{i}", [128, 512]) for i in range(2)]
    tE2 = [c.sb(f"atE2{i}", [128, 512]) for i in range(2)]
    tK.append(c.sb("atK2", [128, 512]))

    def pre0(u):
        nb, di = u // 2, u % 2
        a, g, k = tA[u % 2], tG[u % 2], tK[u % 3]
        if di == 0:
            q = qf[nb % 2]
            ps = self.ps()
            self.proj_fm(ps, w4, 0, 128, nb * 512, 512)
            c.op("act", lambda E: E.copy(q[:], ps[:]), w=[q], r=[ps])
            for tt in range(4):
                tile = nb * 4 + tt
                ps = self.ps()
                self.proj_tm(ps, w4, 128, 128, tile)
                c.op("act", lambda E: E.copy(v_tm[:, tile, :], ps[:, 0:128]), w=[v_tm], r=[ps])
        ps = self.ps()
        self.proj_fm(ps, w4, (2 + di) * 128, 128, nb * 512, 512)
        c.op("act", lambda E: E.activation(a[:], ps[:], AF.Sigmoid), w=[a], r=[ps])
        c.op("dve", lambda E: E.tensor_scalar(a[:], a[:], om[:], lbt[:], ALU.mult, ALU.add), w=[a], r=[a, om, lbt])
        c.op("act", lambda E: E.activation(g[:], a[:], AF.Ln), w=[g], r=[a])
        c.op("pool", lambda E: E.tensor_scalar(k[:], a[:], -1.0, 1.0, ALU.mult, ALU.add), w=[k], r=[a])

    def pre1(u):
        nb, di = u // 2, u % 2
        d = "fb"[di]
        a, g, b, e1, e2 = tA[u % 2], tG[u % 2], tB[u % 2], tE1[u % 2], tE2[u % 2]
        c.op("dve", lambda E: E.tensor_tensor_scan(b[:], scanm, g[:], 0.0, ALU.mult, ALU.add), w=[b], r=[g, self.cf])
        b3 = b[:].rearrange("p (c n) -> p c n", n=64)
        if d == "f":
            c.op("pool", lambda E: E.tensor_tensor(a[:].rearrange("p (c n) -> p c n", n=64), b3, b3[:, :, 63:64].broadcast_to([128, 8, 64]), ALU.subtract), w=[a], r=[b])
        else:
            c.op("pool", lambda E: E.tensor_tensor(a[:], g[:], b[:], ALU.subtract), w=[a], r=[g, b])
        c.op("act", lambda E: E.activation(ebl[d][:, nb * 8:(nb + 1) * 8], b3[:, :, 63], AF.Exp), w=[ebl[d]], r=[b])
        c.op("act", lambda E: E.activation(e2[:], a[:], AF.Exp), w=[e2], r=[a])
        c.op("act", lambda E: E.activation(e1[:], a[:], AF.Exp, scale=-1.0), w=[e1], r=[a])

    def pre2(u):
        nb, di = u // 2, u % 2
        d = "fb"[di]
        blk = slice(nb * 512, (nb + 1) * 512)
        q, k, e1, e2 = qf[nb % 2], tK[u % 3], tE1[u % 2], tE2[u % 2]
        c.op("dve", lambda E: E.tensor_tensor(qt[d][:, blk], q[:], e2[:], ALU.mult), w=[qt[d]], r=[q, e2])
        c.op("pool", lambda E: E.tensor_tensor(kt[d][:, blk], k[:], e1[:], ALU.mult), w=[kt[d]], r=[k, e1])
        pT = self.pst()
        for tt in range(4):
            c.op("pe", lambda E: E.transpose(pT[:, tt * 128:(tt + 1) * 128], kt[d][:, nb * 512 + tt * 128:nb * 512 + (tt + 1) * 128], ident), w=[pT], r=[kt[d], self.cb])
        c.op("act", lambda E: E.copy(ktm[d][:, nb * 4:(nb + 1) * 4, :], pT[:, 0:512].rearrange("p (t n) -> p t n", t=4)), w=[ktm[d]], r=[pT])

    pipeline(2 * NB, [pre0, pre1, pre2])
    self.release(mpre)
    self.tm("A.pre_done")
    Ssn = c.sb("aSsn", [128, 2, NCH, 128], BF16)
    NSB = 3
    m_s = self.mark()
    S = {d: [c.sb(f"aS{d}{i}", [128, 128]) for i in range(NSB)] for d in "fb"}
    for i in range(NCH):
        for di, (d, ch) in enumerate((("f", i), ("b", NCH - 1 - i))):
            tile, cc = ch // 2, ch % 2
            rows = slice(cc * 64, (cc + 1) * 64)
            s_old, s_new = S[d][(i - 1) % NSB], S[d][i % NSB]
            pu = self.ps()
            c.op("pe", lambda E: E.matmul(pu[:, 0:128], ktm[d][rows, tile, :], v_tm[rows, tile, :], start=True, stop=True), w=[pu], r=[ktm[d], v_tm])
            if i == 0:
                c.op("pool", lambda E: E.memset(Ssn[:, di, ch, :], 0.0), w=[Ssn])
                c.op("dve", lambda E: E.tensor_copy(s_new[:], pu[:, 0:128]), w=[s_new], r=[pu])
            else:
                c.op("act", lambda E: E.activation(Ssn[:, di, ch, :], s_old[:], AF.Copy, scale=ebl[d][:, ch:ch + 1]), w=[Ssn], r=[s_old, ebl[d]])
                c.op("dve", lambda E: E.scalar_tensor_tensor(s_new[:], s_old[:], ebl[d][:, ch:ch + 1], pu[:, 0:128], ALU.mult, ALU.add), w=[s_new], r=[s_old, ebl[d], pu])
    self.release(m_s)
    self.tm("A.pass1_done")
    AT = {d: [c.sb(f"aAT{d}{i}", [128, 128], BF16) for i in range(3)] for d in "fb"}
    osum = [c.sb(f"aos{i}", [128, 512]) for i in range(2)]
    sq = [c.sb(f"asq{i}", [128, 512], BF16) for i in range(1)] * 2
    rsb = [c.sb(f"ars{i}", [128, 512]) for i in range(1)] * 2
    yyb = rsb
    szb = [c.sb(f"asz{i}", [128, 512]) for i in range(1)] * 2
    yb = [c.sb(f"ayb{i}", [128, 512], BF16) for i in range(1)] * 2
    def o0(tile):
        tk = slice(tile * 128, (tile + 1) * 128)
        for d in "fb":
            pa = self.ps()
            c.op("pe", lambda E: E.matmul(pa[:, 0:128], kt[d][:, tk], qt[d][:, tk], start=True, stop=True), w=[pa], r=[kt[d], qt[d]])
            at = AT[d][tile % 3]
            c.op("dve", lambda E: E.tensor_tensor(at[:], pa[:, 0:128], mask[d], ALU.mult), w=[at], r=[pa, self.cf])

    def o1(tile):
        nb, tt = tile // 4, tile % 4
        ob = osum[nb % 2]
        po = self.ps()
        for cc in range(2):
            ch = tile * 2 + cc
            rows = slice(cc * 64, (cc + 1) * 64)
            for di, d in enumerate("fb"):
                at = AT[d][tile % 3]
                c.op("pe", lambda E: E.matmul(po[:, rows], v_tm[rows, tile, :], at[rows, rows], start=(di == 0), stop=False), w=[po], r=[v_tm, at])
                c.op("pe", lambda E: E.matmul(po[:, rows], Ssn[:, di, ch, :], qt[d][:, tile * 128 + cc * 64:tile * 128 + (cc + 1) * 64], start=False, stop=(di == 1)), w=[po], r=[Ssn, qt[d]])
        c.op("act", lambda E: E.copy(ob[:, tt * 128:(tt + 1) * 128], po[:, 0:128]), w=[ob], r=[po])
        if tt != 3:
            return
        blk = slice(nb * 512, (nb + 1) * 512)
        s2, rs, yy, sz, ybb = sq[nb % 2], rsb[nb % 2], yyb[nb % 2], szb[nb % 2], yb[nb % 2]
        c.op("pool", lambda E: E.tensor_tensor(s2[:], ob[:], ob[:], ALU.mult), w=[s2], r=[ob])
        pss = self.ps()
        c.op("pe", lambda E: E.matmul(pss[:], onesb, s2[:], start=True, stop=True), w=[pss], r=[self.cb, s2])
        c.op("act", lambda E: E.activation(rs[:], pss[:], AF.Sqrt, bias=self.epsb[:], scale=1.0 / 128), w=[rs], r=[pss, self.epsb])
        c.op("dve", lambda E: E.reciprocal(rs[:], rs[:]), w=[rs], r=[rs])
        c.op("dve", lambda E: E.scalar_tensor_tensor(yy[:], ob[:], gA, rs[:], ALU.mult, ALU.mult), w=[yy], r=[ob, self.ppt, rs])
        ps = self.ps()
        self.proj_fm(ps, wz, 0, 128, nb * 512, 512)
        c.op("act", lambda E: E.activation(sz[:], ps[:], AF.Silu), w=[sz], r=[ps])
        c.op("pool", lambda E: E.tensor_tensor(ybb[:], yy[:], sz[:], ALU.mult), w=[ybb], r=[yy, sz])
        r0 = cfg.yA + h * 128
        c.dma("sp", self.yT_d[r0:r0 + 128, blk], ybb[:], w=[self.yT_d], r=[ybb])

    for step in range(NT + 2):
        if 0 <= step - 2 < NT:
            o1(step - 2)
        if step < NT:
            o0(step)


def load_w_into(self, dst, c0, src_ap, ncols):
    c = self.c
    for o in range(0, ncols, 128):
        n = min(128, ncols - o)
        st = self.wst[self.wsti % 2]
        e = ("dve", "pool")[self.wsti % 2]
        self.wsti += 1
        c.dma("sp", st[:, :, 0:n], src_ap[:, o:o + n].rearrange("(c p) n -> p c n", p=128), w=[st], r=[self.w_in_d, self.w_kv_d])
        c.op(e, lambda E: E.tensor_copy(dst[:, :, c0 + o:c0 + o + n], st[:, :, 0:n]), w=[dst], r=[st])


Gen.phase_A = phase_A
Gen.phase_A_head = phase_A_head
Gen.load_w_into = load_w_into


def phase_C(self, l):
    for p in range(self.cfg.HC // 2):
        m = self.mark()
        self.phase_C_pair(l, p)
        self.release(m)


def phase_C_pair(self, l, P):
    cfg, c = self.cfg, self.c
    T, NT, NB, HC = cfg.T, cfg.NT, cfg.NB, 2
    G2 = 4
    wq = c.sb("cwq", [128, 8, 128], BF16)
    wk = c.sb("cwk", [128, 8, 128], BF16)
    wv = c.sb("cwv", [128, 8, 256], BF16)
    wz2 = c.sb("cwz2", [128, 8, 256], BF16)
    wg = c.sb("cwg", [128, 8, 8], BF16)
    for (wb, n, w) in ((wq, "C_q", 128), (wk, "C_k", 128), (wv, "C_v", 256), (wg, "C_g", 8)):
        o, _ = cfg.col[n]
        self.load_w(wb, self.w_in_d[l][:, o + P * w:o + (P + 1) * w], w)
    qc = c.sb("cqc", [128, T], BF16)
    kc = c.sb("ckc", [128, T], BF16)
    ktm = c.sb("cktm", [128, NT, 2, 128], BF16)
    vp = c.sb("cvp", [128, NT, 2, 130], BF16)
    eb8 = c.sb("ceb8", [128, NT, G2])
    ew = c.sb("cew", [128, NT, G2])
    ebl = c.sb("cebl", [128, NT, G2])
    eblp = c.sb("ceblp", [128, NT, 2])
    Cb = c.sb("cCb", [128, 2, NT, 130], BF16)
    ident = self.cst("ident", True)
    bo, _ = cfg.rp[("gb", l)]
    bo, bw = bo + P * 8, 8
    m_conv = self.mark()
    raw = c.sb("craw", [128, T + 4])
    acc = [c.sb(f"cacc{i}", [128, 512]) for i in range(2)]
    c.op("pool", lambda E: E.memset(raw[:, 0:2], 0.0), w=[raw])
    c.op("pool", lambda E: E.memset(raw[:, T + 2:T + 4], 0.0), w=[raw])
    ai = 0
    for which, (wb, dst) in enumerate(((wq, qc), (wk, kc))):
        for nb in range(NB):
            ps = self.ps()
            self.proj_fm(ps, wb, 0, 128, nb * 512, 512)
            c.op("act", lambda E: E.copy(raw[:, 2 + nb * 512:2 + (nb + 1) * 512], ps[:]), w=[raw], r=[ps])
        wo_ = cfg.pp[("Cw", l, P)] + 5 * which
        for nb in range(NB):
            a = acc[ai % 2]
            ai += 1
            c.op("pool", lambda E: E.tensor_scalar(a[:], raw[:, nb * 512:nb * 512 + 512], self.ppt[:, wo_:wo_ + 1], None, ALU.mult), w=[a], r=[raw, self.ppt])
            for j in range(1, 5):
                c.op("dve", lambda E: E.scalar_tensor_tensor(a[:], raw[:, nb * 512 + j:nb * 512 + j + 512], self.ppt[:, wo_ + j:wo_ + j + 1], a[:], ALU.mult, ALU.add), w=[a], r=[raw, self.ppt, a])
            c.op("act", lambda E: E.activation(dst[:, nb * 512:(nb + 1) * 512], a[:], AF.Silu), w=[dst], r=[a])
    self.release(m_conv)
    self.tm("C.conv_done")
    c.op("pool", lambda E: E.memset(ktm[:], 0.0), w=[ktm])
    for t0 in range(0, NT, 8):
        n = min(8, NT - t0)
        pT = self.pst()
        for tt in range(n):
            c.op("pe", lambda E: E.transpose(pT[:, tt * 128:(tt + 1) * 128], kc[:, (t0 + tt) * 128:(t0 + tt + 1) * 128], ident), w=[pT], r=[kc, self.cb])
        pv = pT[:, 0:n * 128].rearrange("p (t n) -> p t n", t=n)
        c.op("act", lambda E: E.copy(ktm[:, t0:t0 + n, 0, 0:64], pv[:, :, 0:64]), w=[ktm], r=[pT])
        c.op("dve", lambda E: E.tensor_copy(ktm[:, t0:t0 + n, 1, 64:128], pv[:, :, 64:128]), w=[ktm], r=[pT])
    c.op("pool", lambda E: E.memset(vp[:, :, :, 128:130], 1.0), w=[vp])
    triU, triL, onesf = self.cst("triU"), self.cst("triL"), self.cst("ones")
    gall = c.sb("cgall", [128, NT, 8])
    fall = c.sb("cfall", [128, NT, G2])
    tgall = c.sb("ctgall", [128, NT, G2])
    for tile in range(NT):
        ps = self.ps()
        self.proj_tm(ps, wv, 0, 256, tile)
        c.op("act", lambda E: E.copy(vp[:, tile, :, 0:128], ps[:, 0:256].rearrange("p (h n) -> p h n", h=2)), w=[vp], r=[ps])
        ps = self.ps()
        self.proj_tm(ps, wg, 0, 8, tile)
        c.op("dve", lambda E: E.tensor_tensor(gall[:, tile, :], ps[:, 0:8], self.rowt[:, bo:bo + bw], ALU.add), w=[gall], r=[ps, self.rowt])
    c.op("act", lambda E: E.activation(fall[:], gall[:, :, G2:2 * G2], AF.Sigmoid), w=[fall], r=[gall])
    c.op("act", lambda E: E.activation(fall[:], fall[:], AF.Ln), w=[fall], r=[fall])
    pc = self.ps()
    pc3 = pc[:, 0:NT * 8].rearrange("p (t g) -> p t g", g=8)
    for tile in range(NT):
        c.op("pe", lambda E: E.matmul(pc[:, tile * 8:tile * 8 + 2], triU, fall[:, tile, 0:2], start=True, stop=True), w=[pc], r=[self.cf, fall])
        c.op("pe", lambda E: E.matmul(pc[:, tile * 8 + 2:tile * 8 + 4], triL, fall[:, tile, 2:4], start=True, stop=True), w=[pc], r=[self.cf, fall])
        c.op("pe", lambda E: E.matmul(pc[:, tile * 8 + 4:tile * 8 + 8], onesf, fall[:, tile, 0:4], start=True, stop=True), w=[pc], r=[self.cf, fall])
    c.op("act", lambda E: E.activation(tgall[:], pc3[:, :, 0:G2], AF.Exp), w=[tgall], r=[pc])
    c.op("pool", lambda E: E.tensor_scalar(eb8[:], tgall[:], 0.125, None, ALU.mult), w=[eb8], r=[tgall])
    c.op("dve", lambda E: E.tensor_tensor(tgall[:], gall[:, :, 0:G2], pc3[:, :, 0:G2], ALU.subtract), w=[tgall], r=[gall, pc, tgall])
    c.op("act", lambda E: E.activation(ew[:], tgall[:], AF.Exp), w=[ew], r=[tgall])
    c.op("act", lambda E: E.activation(ebl[:], pc3[:, :, G2:2 * G2], AF.Exp), w=[ebl], r=[pc])
    import os
    CSTOP = int(os.environ.get("CSTOP", 9))
    if CSTOP <= 1:
        return
    for hh in range(2):
        rows = slice(hh * 64, hh * 64 + 64)
        c.op("pool", lambda E: E.tensor_copy(eblp[rows, :, :], ebl[rows, :, hh:4:2]), w=[eblp], r=[ebl])

    def mk_v2(dst, tile):
        c.op("pool", lambda E: E.tensor_tensor(dst[:], vp[:, tile].unsqueeze(1).broadcast_to([128, 2, 2, 130]),
                                                ew[:, tile, :].rearrange("p (d h) -> p d h", d=2).unsqueeze(3).broadcast_to([128, 2, 2, 130]), ALU.mult),
             w=[dst], r=[vp, ew])

    if CSTOP <= 2:
        return
    self.tm("C.gates_done")
    V2 = [c.sb(f"cV2{i}", [128, 2, 2, 130], BF16) for i in range(3)]
    Dd = [c.sb(f"cD{d}", [128, 130]) for d in range(2)]
    c.op("pool", lambda E: E.memset(Cb[:, 0, 0, :], 0.0), w=[Cb])
    c.op("pool", lambda E: E.memset(Cb[:, 1, NT - 1, :], 0.0), w=[Cb])
    vi = 0
    for i in range(NT):
        for d, tile, prev, nxt in ((0, i, i - 1, i + 1), (1, NT - 1 - i, NT - i, NT - 2 - i)):
            v2 = V2[vi % 3]
            vi += 1
            mk_v2(v2, tile)
            pu = self.ps()
            c.op("pe", lambda E: E.matmul(pu[:, 0:129], ktm[:, tile, 0, :], v2[:, d, 0, 0:129], start=True, stop=False), w=[pu], r=[ktm, v2])
            c.op("pe", lambda E: E.matmul(pu[:, 0:129], ktm[:, tile, 1, :], v2[:, d, 1, 0:129], start=False, stop=True), w=[pu], r=[ktm, v2])
            if i == 0:
                c.op("dve", lambda E: E.tensor_copy(Dd[d][:, 0:129], pu[:, 0:129]), w=[Dd[d]], r=[pu])
            else:
                c.op("dve", lambda E: E.scalar_tensor_tensor(Dd[d][:, 0:129], Dd[d][:, 0:129], eblp[:, prev, d:d + 1], pu[:, 0:129], ALU.mult, ALU.add), w=[Dd[d]], r=[Dd[d], eblp, pu])
            if 0 <= nxt < NT:
                c.op("act", lambda E: E.activation(Cb[:, d, nxt, 0:129], Dd[d][:, 0:129], AF.Copy, scale=eblp[:, tile, d:d + 1]), w=[Cb], r=[Dd[d], eblp])
    if CSTOP <= 3:
        return
    self.tm("C.pass1_done")
    oo, _ = cfg.col["C_o"]
    zo, _ = cfg.col["C_z"]
    self.load_w(wv, self.w_in_d[l][:, oo + P * 256:oo + (P + 1) * 256], 256)
    self.load_w(wz2, self.w_in_d[l][:, zo + P * 256:zo + (P + 1) * 256], 256)
    AT = [c.sb(f"cAT{i}", [128, 2, 2, 128], BF16) for i in range(3)]
    nsb = [c.sb(f"cns{i}", [128, 2, 2, 129]) for i in range(2)]
    dn = [c.sb(f"cdn{i}", [128, 2, 2]) for i in range(2)]
    hh4 = [c.sb(f"chh{i}", [128, 2, 2, 128]) for i in range(2)]
    hs_ = [c.sb(f"chs{i}", [128, 256]) for i in range(2)]
    so = [c.sb(f"cso{i}", [128, 256]) for i in range(2)]
    st6 = [c.sb(f"cst6{i}", [128, 2, 6]) for i in range(2)]
    mv = [c.sb(f"cmv{i}", [128, 2, 2]) for i in range(2)]
    rs = [c.sb(f"crs{i}", [128, 2]) for i in range(2)]
    ytm = [c.sb(f"cytm{i}", [128, 256], BF16) for i in range(2)]
    yT = [c.sb(f"cyT{i}", [128, 2, 128], BF16) for i in range(2)]
    go, _ = cfg.rp[("gC", l)]
    go, gw = go + P * 256, 256
    mo = cfg.cc["triU"][0]
    masks = self.cf[:, mo:mo + 256].rearrange("p (d n) -> p d n", d=2).unsqueeze(2).broadcast_to([128, 2, 2, 128])
    masks3 = self.cf[:, mo:mo + 256].rearrange("p (d n) -> p d n", d=2)

    def bufs(tile):
        return (AT[tile % 3], V2[tile % 3], nsb[tile % 2], dn[tile % 2], hh4[tile % 2], hs_[tile % 2], so[tile % 2],
                st6[tile % 2], mv[tile % 2], rs[tile % 2], ytm[tile % 2], yT[tile % 2])

    def s0(tile):
        tk = slice(tile * 128, (tile + 1) * 128)
        at, v2 = bufs(tile)[0:2]
        for h in range(2):
            rows = slice(h * 64, h * 64 + 64)
            pst_ = self.ps()
            c.op("pe", lambda E: E.matmul(pst_[:, 0:128], kc[rows, tk], qc[rows, tk], start=True, stop=True), w=[pst_], r=[kc, qc])
            c.op("dve", lambda E: E.tensor_tensor(at[:, :, h, :], pst_[:, 0:128].unsqueeze(1).broadcast_to([128, 2, 128]), masks3, ALU.mult), w=[at], r=[pst_, self.cf])
        mk_v2(v2, tile)

    def s1(tile):
        if CSTOP <= 4:
            return
        tk = slice(tile * 128, (tile + 1) * 128)
        at, v2, ns, dd, h4, a = bufs(tile)[0:6]
        for d in range(2):
            for h in range(2):
                rows = slice(h * 64, h * 64 + 64)
                j = d * 2 + h
                pn = self.ps()
                c.op("pe", lambda E: E.matmul(pn[:, 0:129], at[:, d, h, :], v2[:, d, h, 0:129], start=True, stop=False), w=[pn], r=[at, v2])
                c.op("pe", lambda E: E.matmul(pn[:, 0:129], qc[rows, tk], Cb[rows, d, tile, 0:129], start=False, stop=True), w=[pn], r=[qc, Cb])
                c.op("act", lambda E: E.activation(ns[:, d, h, :], pn[:, 0:129], AF.Copy, scale=eb8[:, tile, j:j + 1]), w=[ns], r=[pn, eb8])
        c.op("dve", lambda E: E.tensor_scalar(dd[:], ns[:, :, :, 128], -1.0, 1.0, ALU.mult, ALU.max), w=[dd], r=[ns])
        c.op("dve", lambda E: E.tensor_tensor(dd[:], dd[:], ns[:, :, :, 128], ALU.max), w=[dd], r=[dd, ns])
        c.op("dve", lambda E: E.reciprocal(dd[:], dd[:]), w=[dd], r=[dd])
        c.op("pool", lambda E: E.tensor_tensor(h4[:], ns[:, :, :, 0:128], dd[:].unsqueeze(3).broadcast_to([128, 2, 2, 128]), ALU.mult), w=[h4], r=[ns, dd])
        c.op("pool", lambda E: E.tensor_tensor(a[:].rearrange("p (h n) -> p h n", h=2), h4[:, 0], h4[:, 1], ALU.add), w=[a], r=[h4])

    def s2(tile):
        if CSTOP <= 5:
            return
        a, s_, s6, m2, r2, yt = bufs(tile)[5:11]
        ps = self.ps()
        self.proj_tm(ps, wv, 0, 256, tile)
        c.op("act", lambda E: E.activation(s_[:], ps[:, 0:256], AF.Sigmoid), w=[s_], r=[ps])
        c.op("dve", lambda E: E.tensor_tensor(a[:], a[:], s_[:], ALU.mult), w=[a], r=[a, s_])
        for h in range(2):
            c.op("dve", lambda E: E.bn_stats(s6[:, h, :], a[:, h * 128:(h + 1) * 128]), w=[s6], r=[a])
            c.op("dve", lambda E: E.bn_aggr(m2[:, h, :], s6[:, h, :]), w=[m2], r=[s6])
        c.op("pool", lambda E: E.tensor_scalar(r2[:], m2[:, :, 1], EPS, None, ALU.add), w=[r2], r=[m2])
        c.op("pool", lambda E: E.tensor_tensor(r2[:], r2[:], self.negh[:, 0:2], ALU.pow), w=[r2], r=[r2, self.negh])
        for h in range(2):
            c.op("dve", lambda E: E.tensor_scalar(a[:, h * 128:(h + 1) * 128], a[:, h * 128:(h + 1) * 128], m2[:, h, 0:1], r2[:, h:h + 1], ALU.subtract, ALU.mult), w=[a], r=[a, m2, r2])
        c.op("pool", lambda E: E.tensor_tensor(a[:], a[:], self.rowt[:, go:go + gw], ALU.mult), w=[a], r=[a, self.rowt])
        ps = self.ps()
        self.proj_tm(ps, wz2, 0, 256, tile)
        c.op("act", lambda E: E.activation(s_[:], ps[:, 0:256], AF.Sigmoid), w=[s_], r=[ps])
        c.op("pool", lambda E: E.tensor_tensor(a[:], a[:], s_[:], ALU.mult), w=[a], r=[a, s_])
        c.op("dve", lambda E: E.tensor_tensor(yt[:], ps[:, 0:256], a[:], ALU.mult), w=[yt], r=[ps, a])

    def s3(tile):
        if CSTOP <= 6:
            return
        tk = slice(tile * 128, (tile + 1) * 128)
        yt, yo = bufs(tile)[10:12]
        pT = self.pst()
        for h in range(2):
            c.op("pe", lambda E: E.transpose(pT[:, h * 128:(h + 1) * 128], yt[:, h * 128:(h + 1) * 128], ident), w=[pT], r=[yt, self.cb])
        c.op("act", lambda E: E.copy(yo[:], pT[:, 0:256].rearrange("p (h n) -> p h n", h=2)), w=[yo], r=[pT])
        c.dma("sp", self.yT_d[cfg.yC + P * 256:cfg.yC + (P + 1) * 256, tk].rearrange("(h p) n -> p h n", p=128), yo[:], w=[self.yT_d], r=[yo])

    for step in range(NT + 4):
        for fn, lag in ((s3, 4), (s2, 3), (s1, 2), (s0, 0)):
            if 0 <= step - lag < NT:
                fn(step - lag)


Gen.phase_C_pair = phase_C_pair
Gen.phase_C = phase_C


_CACHE = {}
_SELS = [([0, 1], [0], [0, 1], [0, 1]), ([2, 3], [1], [2, 3], [2, 3])]


def kernel(**inputs):
    inp = {k: np.asarray(v) for k, v in inputs.items()}
    B, T = inp["x"].shape[0], inp["x"].shape[1]
    cfg = Cfg(T, 2, 1, 2, 2)
    pks = [pack_weights(cfg, inp, *s) for s in _SELS]
    if "nc" not in _CACHE:
        _CACHE["nc"] = Gen(cfg, phases="NDBACO", mode="split").build()
    nc = _CACHE["nc"]
    in_maps = []
    for b in range(B):
        for h in range(2):
            m = dict(pks[h])
            m["x"] = np.ascontiguousarray(inp["x"][b], dtype=np.float32)
            m["mem"] = np.ascontiguousarray(inp["mem"][b], dtype=np.float32)
            m["pos"] = np.ascontiguousarray(inp["positions"][b:b + 1], dtype=np.int32)
            in_maps.append(m)
    res = run_bass_kernel_spmd(nc, in_maps, core_ids=list(range(2 * B)))
    return np.stack([np.asarray(res.results[2 * b]["out"], dtype=np.float32) for b in range(B)], axis=0)
```

```python
import math
from concourse.bass_utils import run_bass_kernel_spmd
import numpy as np
import concourse.bass as bass
import concourse.mybir as mybir

F32 = mybir.dt.float32
BF16 = mybir.dt.bfloat16
I32 = mybir.dt.int32
ALU = mybir.AluOpType
AF = mybir.ActivationFunctionType
AX = mybir.AxisListType

NDS = 40


class Buf:
    def __init__(self, t, name):
        self.t = t
        self.name = name
        self.w = None
        self.r = {}

    def __getitem__(self, idx):
        return self.t[idx]


class Ctx:
    def __init__(self, nc):
        self.nc = nc
        self.E = {"pe": nc.tensor, "act": nc.scalar, "dve": nc.vector, "pool": nc.gpsimd, "sp": nc.sync}
        self.sem = {k: nc.alloc_semaphore("s_" + k) for k in self.E}
        self.cnt = {k: 0 for k in self.E}
        self.seen = {k: {} for k in self.E}
        self.dsem = [nc.alloc_semaphore(f"d{i}") for i in range(NDS)]
        self.dval = [0] * NDS
        self.di = 0
        self.guards = []
        self.nops = 0

    def sb(self, name, shape, dt=F32):
        self.uid = getattr(self, "uid", 0) + 1
        name = f"{name}_{self.uid}"
        g = self.nc.sbuf_tensor(name, list(shape), dt)
        t = g.__enter__()
        self.guards.append(g)
        return Buf(t, name)

    def ps(self, name, shape, dt=F32):
        self.uid = getattr(self, "uid", 0) + 1
        name = f"{name}_{self.uid}"
        g = self.nc.psum_tensor(name, list(shape), dt)
        t = g.__enter__()
        self.guards.append(g)
        return Buf(t, name)

    def dram(self, name, shape, dt=F32, kind="Internal"):
        t = self.nc.dram_tensor(name, list(shape), dt, kind=kind)
        return Buf(t, name)

    def _wait(self, eng, tok):
        kind, key, val = tok
        k = (kind, key)
        if self.seen[eng].get(k, 0) >= val:
            return
        semh = self.sem[key] if kind == "e" else (self.dsem[key] if kind == "d" else self.csem[key])
        self.E[eng].wait_ge(semh, val)
        self.seen[eng][k] = val

    def _deps(self, eng, w, r):
        toks = []
        for b in r:
            if b.w is not None:
                toks.append(b.w)
        for b in w:
            if b.w is not None:
                toks.append(b.w)
            for k, v in b.r.items():
                toks.append((k[0], k[1], v))
        for tok in toks:
            if eng == "pe" and tok[0] == "e" and tok[1] == "pe":
                continue
            self._wait(eng, tok)

    def _mark(self, tok, w, r):
        k = (tok[0], tok[1])
        for b in r:
            if b.r.get(k, 0) < tok[2]:
                b.r[k] = tok[2]
        for b in w:
            b.w = tok
            b.r = {}

    def op(self, eng, fn, w=(), r=()):
        self._deps(eng, w, r)
        ins = fn(self.E[eng])
        self.cnt[eng] += 1
        ins.then_inc(self.sem[eng], 1)
        tok = ("e", eng, self.cnt[eng])
        self._mark(tok, w, r)
        self.nops += 1
        return tok

    def dma(self, q, out_ap, in_ap, w=(), r=(), **kw):
        import os
        if os.environ.get("ALLSP"):
            q = "sp"
        self._deps(q, w, r)
        slot = self.di % NDS
        self.di += 1
        if self.dval[slot] > 0:
            self._wait(q, ("d", slot, self.dval[slot]))
        self.dval[slot] += 16
        self.E[q].dma_start(out=out_ap, in_=in_ap, **kw).then_inc(self.dsem[slot], 16)
        tok = ("d", slot, self.dval[slot])
        self._mark(tok, w, r)
        self.nops += 1
        return tok

    def collective(self, kind, alu, groups, in_buf, out_buf):
        if not hasattr(self, "csem"):
            self.csem = []
        self._deps("pool", [out_buf], [in_buf])
        sem = self.nc.alloc_semaphore(f"cc{len(self.csem)}")
        self.csem.append(sem)
        self.nc.gpsimd.collective_compute(kind, alu, replica_groups=groups, ins=[in_buf.t.ap().opt()], outs=[out_buf.t.ap().opt()]).then_inc(sem)
        tok = ("c", len(self.csem) - 1, 1)
        self._mark(tok, [out_buf], [in_buf])
        return tok

    def finish(self, bufs, eng="sp"):
        for b in bufs:
            if b.w is not None:
                self._wait(eng, b.w)

    def close(self):
        for g in reversed(self.guards):
            g.__exit__(None, None, None)
        self.guards = []


D = 1024
EPS = 1e-6


class Cfg:
    def __init__(self, T, HA, GB, HC, HD, L=2, MEM=256):
        self.T, self.HA, self.GB, self.HC, self.HD, self.L, self.MEM = T, HA, GB, HC, HD, L, MEM
        self.NT = T // 128
        self.NB = T // 512
        cols = [("A_q", HA * 128), ("A_i", HA * 128), ("A_ff", HA * 128), ("A_fb", HA * 128), ("A_z", HA * 128),
                ("B_q", GB * 256), ("B_k", GB * 64), ("B_v", GB * 64), ("B_z", GB * 256),
                ("C_q", HC * 64), ("C_k", HC * 64), ("C_v", HC * 128), ("C_o", HC * 128), ("C_z", HC * 128),
                ("C_g", 4 * HC), ("D_q", HD * 128), ("D_z", HD * 128)]
        self.col = {}
        o = 0
        for n, w in cols:
            self.col[n] = (o, w)
            o += w
        self.NIN = o
        self.yA, self.yB, self.yC, self.yD = 0, HA * 128, HA * 128 + GB * 256, HA * 128 + GB * 256 + HC * 128
        self.NY = self.yD + HD * 128
        self.pp = {}
        o = 0
        for l in range(L):
            for h in range(HA):
                self.pp[("A", l, h)] = o
                o += 3
            for p in range(HC // 2):
                self.pp[("Cw", l, p)] = o
                o += 10
        self.NPP = o
        self.rp = {}
        o = 0
        for l in range(L):
            for n, w in (("gC", HC * 128), ("gb", 4 * HC), ("sink", GB * 4)):
                self.rp[(n, l)] = (o, w)
                o += w
        self.NRS = o
        for l in range(L):
            for n, w in (("ng", D), ("mg", D)):
                self.rp[(n, l)] = (o, w)
                o += w
        self.rp[("fg", 0)] = (o, D)
        o += D
        self.NRP = o
        self.cc = {}
        o = 0
        for n, w in (("ident", 128), ("triU", 128), ("triL", 128), ("bdU", 128), ("bdL", 128), ("P64T", 64),
                     ("invf", 1), ("scanm", 512), ("ones", 128)):
            self.cc[n] = (o, w)
            o += w
        self.NCC = o


def make_consts(cfg):
    c = np.zeros((128, cfg.NCC), np.float32)

    def put(n, a):
        o, w = cfg.cc[n]
        c[: a.shape[0], o:o + w] = a

    s = np.arange(128)[:, None]
    t = np.arange(128)[None, :]
    put("ident", np.eye(128, dtype=np.float32))
    put("triU", (s <= t).astype(np.float32))
    put("triL", (s >= t).astype(np.float32))
    same = (s // 64) == (t // 64)
    put("bdU", ((s <= t) & same).astype(np.float32))
    put("bdL", ((s >= t) & same).astype(np.float32))
    P = np.zeros((64, 64), np.float32)
    for i in range(8):
        P[i + 8, i] = -1.0
        P[i, i + 8] = 1.0
    put("P64T", P)
    invf = (500000.0 ** (-np.arange(0, 16, 2, dtype=np.float32) / 16.0)).astype(np.float32)
    iv = np.zeros((128, 1), np.float32)
    iv[0:8, 0] = invf / np.float32(2 * np.pi)
    iv[8:16, 0] = invf / np.float32(2 * np.pi)
    put("invf", iv)
    sm = np.ones((128, 512), np.float32)
    sm[:, ::64] = 0.0
    put("scanm", sm)
    put("ones", np.ones((128, 128), np.float32))
    return c


class BfView:
    def __init__(self, b):
        object.__setattr__(self, "_b", b)

    def __getitem__(self, idx):
        return self._b.t[:].bitcast(BF16)[idx]

    def __getattr__(self, k):
        return getattr(self._b, k)

    def __setattr__(self, k, v):
        setattr(self._b, k, v)


def pipeline(n, stages):
    S = len(stages)
    for step in range(n + S - 1):
        for s_ in range(S - 1, -1, -1):
            i = step - s_
            if 0 <= i < n:
                stages[s_](i)


class Gen:
    def __init__(self, cfg, phases="NDBACO", mode="full", layers=None, debug_out=None):
        self.cfg = cfg
        self.phases = phases
        self.mode = mode
        self.layers = list(range(cfg.L)) if layers is None else layers
        self.debug_out = debug_out
        nc = bass.Bass("TRN2", target_bir_lowering=False)
        self.nc = nc
        self.c = Ctx(nc)

    def barrier(self):
        c = self.c
        for e in ("pe", "act", "dve", "pool", "sp"):
            for o in ("pe", "act", "dve", "pool"):
                if o != e and c.cnt[o] > 0:
                    c._wait(e, ("e", o, c.cnt[o]))
            for s in range(NDS):
                if c.dval[s] > 0:
                    c._wait(e, ("d", s, c.dval[s]))
            for i in range(len(getattr(c, "csem", []))):
                c._wait(e, ("c", i, 1))

    def ar_src(self, li):
        nar = self.cfg.T // 1024

        def fn(t):
            b = self.arout[li * nar + t // 8]
            return b[(t % 8) * 128:(t % 8 + 1) * 128, :], [b]
        return fn

    def tm(self, name):
        if not hasattr(self, "tmarks"):
            self.tmarks = []
        self.tmarks.append((name, dict(self.c.cnt)))

    def mark(self):
        return len(self.c.guards)

    def release(self, m):
        self.barrier()
        c = self.c
        while len(c.guards) > m:
            c.guards.pop().__exit__(None, None, None)

    def ps(self):
        b = self.psf[self.psi % len(self.psf)]
        self.psi += 1
        return b

    def pst(self):
        return BfView(self.ps())

    def cst(self, n, bf=False, rows=128):
        o, w = self.cfg.cc[n]
        t = self.cb if bf else self.cf
        return t[0:rows, o:o + w]

    def load_w(self, dst, src_ap, ncols, eng_cycle=("dve", "pool")):
        c = self.c
        step = 128
        i = 0
        for o in range(0, ncols, step):
            n = min(step, ncols - o)
            st = self.wst[self.wsti % 2]
            self.wsti += 1
            c.dma("sp", st[:, :, 0:n], src_ap[:, o:o + n].rearrange("(c p) n -> p c n", p=128), w=[st], r=[self.w_in_d, self.w_kv_d])
            e = eng_cycle[i % len(eng_cycle)]
            i += 1
            c.op(e, lambda E: E.tensor_copy(dst[:, :, o:o + n], st[:, :, 0:n]), w=[dst], r=[st])

    def proj_fm(self, ps, wb, c0, m, tok0, ntok, rows0=0):
        c = self.c
        for k in range(8):
            c.op("pe", lambda E: E.matmul(ps[0:m, 0:ntok], wb[:, k, c0:c0 + m], self.hT[:, k, tok0:tok0 + ntok], start=(k == 0), stop=(k == 7)),
                 w=[ps], r=[wb, self.hT])

    def proj_tm(self, ps, wb, c0, n, tile, col0=0):
        c = self.c
        for k in range(8):
            c.op("pe", lambda E: E.matmul(ps[:, col0:col0 + n], self.hT[:, k, tile * 128:(tile + 1) * 128], wb[:, k, c0:c0 + n], start=(k == 0), stop=(k == 7)),
                 w=[ps], r=[wb, self.hT])

    def load_grow(self, key):
        o, w = self.cfg.rp[key]
        self.c.dma("sp", self.grow[:], self.rp_d[0:1, o:o + w].partition_broadcast(128), w=[self.grow], r=[self.rp_d])

    def rmsnorm_T(self, src_ap_fn, ntiles, gkey, dstT, src_bufs, add_aps=None):
        c = self.c
        self.load_grow(gkey)
        grow = self.grow[:]
        m_n = self.mark()
        xts = list(self.xt) + ([c.sb(f"xtx{i}", [128, D]) for i in range(2)] if ntiles > 4 else [])
        nx = len(xts)
        hbx = list(self.hbs) + ([c.sb(f"hbx{i}", [128, D], BF16) for i in range(2)] if ntiles > 4 else [])
        nh = len(hbx)

        def s0(t):
            xb = xts[t % nx]
            c.dma("sp", xb[:], src_ap_fn(t), w=[xb], r=src_bufs)
            if add_aps is not None:
                for ai_, fn in enumerate(add_aps):
                    xa = self.xa[(t + ai_) % 2]
                    ap_, bufs = fn(t)
                    c.dma("sp", xa[:], ap_, w=[xa], r=bufs)
                    c.op("pool", lambda E: E.tensor_tensor(xb[:], xb[:], xa[:], ALU.add), w=[xb], r=[xb, xa])

        def s1(t):
            xb, hb, ss, rstd = xts[t % nx], hbx[t % nh], self.sss[t % 2], self.rstds[t % 2]
            c.op("act", lambda E: E.activation(hb[:], xb[:], AF.Square, accum_out=ss[:]), w=[hb, ss], r=[xb])
            c.op("pool", lambda E: E.tensor_scalar(rstd[:], ss[:], 1.0 / D, EPS, ALU.mult, ALU.add), w=[rstd], r=[ss])
            c.op("pool", lambda E: E.tensor_tensor(rstd[:], rstd[:], self.negh[:, 0:1], ALU.pow), w=[rstd], r=[rstd, self.negh])
            c.op("dve", lambda E: E.scalar_tensor_tensor(hb[:], xb[:], rstd[:], grow, ALU.mult, ALU.mult), w=[hb], r=[xb, rstd, self.grow])

        def s2(t):
            hb = hbx[t % nh]
            pT = self.pst()
            for k in range(8):
                c.op("pe", lambda E: E.transpose(pT[:, k * 128:(k + 1) * 128], hb[:, k * 128:(k + 1) * 128], self.cst("ident", True)), w=[pT], r=[hb, self.cb])
            c.op("act", lambda E: E.copy(dstT[:, :, t * 128:(t + 1) * 128], pT[:].rearrange("p (k n) -> p k n", k=8)), w=[dstT], r=[pT])

        lag2 = 3 if nh >= 4 else 2
        for step in range(ntiles + lag2):
            for fn, lag in ((s2, lag2), (s1, 1), (s0, 0)):
                if 0 <= step - lag < ntiles:
                    fn(step - lag)
        if nx > 2:
            self.release(m_n)

    def build(self):
        cfg, c, nc = self.cfg, self.c, self.nc
        T, NT, NB = cfg.T, cfg.NT, cfg.NB
        L = cfg.L
        self.x_d = c.dram("x", [T, D], F32, "ExternalInput")
        self.mem_d = c.dram("mem", [cfg.MEM, D], F32, "ExternalInput")
        self.pos_d = c.dram("pos", [1, T], I32, "ExternalInput")
        self.w_in_d = c.dram("w_in", [L, D, cfg.NIN], F32, "ExternalInput")
        self.w_kv_d = c.dram("w_kv", [L, D, 2 * cfg.HD * 128], F32, "ExternalInput")
        self.w_out_d = c.dram("w_out", [L, cfg.NY, D], F32, "ExternalInput")
        self.pp_d = c.dram("pp", [128, max(cfg.NPP, 1)], F32, "ExternalInput")
        self.rp_d = c.dram("rp", [1, cfg.NRP], F32, "ExternalInput")
        self.cc_d = c.dram("cc", [128, cfg.NCC], F32, "ExternalInput")
        if self.mode == "add2":
            self.pa_d = c.dram("pa", [T, D], F32, "ExternalInput")
            self.pb_d = c.dram("pb", [T, D], F32, "ExternalInput")
        self.out_d = c.dram("out", [T, D], F32, "ExternalOutput")
        self.yT_d = c.dram("yT_s", [cfg.NY, T], BF16, "ExternalOutput" if getattr(self, "dbg_y", False) else "Internal")
        self.x1_d = c.dram("x1_s", [T, D], F32)
        self.hf_d = c.dram("hf_s", [T, 256], F32)
        self.hb_d = c.dram("hb_s", [T, 256], F32)
        if self.debug_out:
            self.dbg_d = c.dram("dbg", list(self.debug_out), F32, "ExternalOutput")
        if self.mode == "split":
            self.groups = [[0, 1], [2, 3], [4, 5], [6, 7]]
            nar = T // 1024
            self.arin = [Buf(nc.dram_tensor(f"arin{i}", [1024, D], F32), f"arin{i}") for i in range(nar * len(self.layers))]
            self.arout = [Buf(nc.dram_tensor(f"arout{i}", [1024, D], F32), f"arout{i}") for i in range(nar * len(self.layers))]

        self.cf = c.sb("cf", [128, cfg.NCC])
        self.cb = c.sb("cb", [128, cfg.NCC], BF16)
        self.ppt = c.sb("ppt", [128, max(cfg.NPP, 1)])
        self.rowt = c.sb("rowt", [128, cfg.NRS])
        self.grow = c.sb("grow", [128, D])
        self.hT = c.sb("hT", [128, 8, T], BF16)
        self.wst = [c.sb(f"wst{i}", [128, 8, 128]) for i in range(2)]
        self.wsti = 0
        self.xt = [c.sb(f"xt{i}", [128, D]) for i in range(2)]
        self.xa = [c.sb(f"xa{i}", [128, D]) for i in range(1)] * 2 if self.mode in ("add2", "split") else None
        self.sss = [c.sb(f"ss{i}", [128, 1]) for i in range(2)]
        self.rstds = [c.sb(f"rstd{i}", [128, 1]) for i in range(2)]
        self.ss, self.rstd = self.sss[0], self.rstds[0]
        self.epsb = c.sb("epsb", [128, 1])
        self.hbs = [c.sb(f"hb{i}", [128, D], BF16) for i in range(2)]
        self.psf = [c.ps(f"psf{i}", [128, 512]) for i in range(8)]
        self.psi = 0

        c.dma("sp", self.cf[:], self.cc_d[:], w=[self.cf], r=[self.cc_d])
        c.dma("sp", self.ppt[:], self.pp_d[:], w=[self.ppt], r=[self.pp_d])
        c.dma("sp", self.rowt[:], self.rp_d[0:1, 0:cfg.NRS].partition_broadcast(128), w=[self.rowt], r=[self.rp_d])
        c.op("dve", lambda E: E.tensor_copy(self.cb[:], self.cf[:]), w=[self.cb], r=[self.cf])
        c.op("pool", lambda E: E.memset(self.epsb[:], EPS), w=[self.epsb])
        self.negh = c.sb("negh", [128, 4])
        c.op("pool", lambda E: E.memset(self.negh[:], -0.5), w=[self.negh])

        if "B" in self.phases:
            self.rope_tables()

        for l in self.layers:
            last = (l == L - 1)
            if l == self.layers[0]:
                src, srcb = (lambda t: self.x_d[t * 128:(t + 1) * 128, :]), [self.x_d]
                adds = None
                if self.mode == "add2":
                    adds = [(lambda t: (self.pa_d[t * 128:(t + 1) * 128, :], [self.pa_d])),
                            (lambda t: (self.pb_d[t * 128:(t + 1) * 128, :], [self.pb_d]))]
            elif self.mode == "split":
                src, srcb = (lambda t: self.x_d[t * 128:(t + 1) * 128, :]), [self.x_d]
                adds = [self.ar_src(li) for li in range(self.layers.index(l))]
            else:
                src, srcb = (lambda t: self.x1_d[t * 128:(t + 1) * 128, :]), [self.x1_d]
                adds = None
            self.xsrc = (src, srcb, adds)
            if "N" in self.phases:
                self.rmsnorm_T(src, NT, ("ng", l), self.hT, srcb, adds)
            if "D" in self.phases:
                m = self.mark()
                self.phase_D(l)
                self.release(m)
            if "B" in self.phases:
                m = self.mark()
                self.phase_B(l)
                self.release(m)
            if "A" in self.phases:
                m = self.mark()
                self.phase_A(l)
                self.release(m)
            if "C" in self.phases:
                m = self.mark()
                self.phase_C(l)
                self.release(m)
            if "O" in self.phases:
                m = self.mark()
                self.phase_O(l, last)
                self.release(m)
        outs = [self.out_d] + ([self.dbg_d] if self.debug_out else [])
        c.finish(outs, "sp")
        c.finish(outs, "pool")
        self.barrier()
        c.close()
        return nc

    def phase_D(self, l):
        cfg, c = self.cfg, self.c
        T, NB, HD = cfg.T, cfg.NB, cfg.HD
        M = cfg.MEM
        memT = c.sb("memT", [128, 8, M], BF16)
        wkv = c.sb("wkv", [128, 8, 2 * HD * 128], BF16)
        KT = c.sb("KT", [128, HD, M], BF16)
        Vt = c.sb("Vt", [128, M // 128, HD * 128], BF16)
        wq = c.sb("wdq", [128, 8, HD * 128], BF16)
        wz = c.sb("wdz", [128, 8, HD * 128], BF16)
        qTb = [c.sb(f"dq{i}", [128, 512], BF16) for i in range(2)]
        szb = [c.sb(f"dsz{i}", [128, 512], BF16) for i in range(2)]
        Eb = [c.sb(f"dE{i}", [128, 512], BF16) for i in range(4)]
        rden = [c.sb(f"drd{i}", [128, 512]) for i in range(2)]
        y1 = [c.sb(f"dy1{i}", [128, 512]) for i in range(2)]
        yb = [c.sb(f"dyb{i}", [128, 512], BF16) for i in range(2)]
        self.rmsnorm_T(lambda t: self.mem_d[t * 128:(t + 1) * 128, :], M // 128, ("mg", l), memT, [self.mem_d])
        self.load_w(wkv, self.w_kv_d[l], 2 * HD * 128)
        qo, _ = cfg.col["D_q"]
        zo, _ = cfg.col["D_z"]
        self.load_w(wq, self.w_in_d[l][:, qo:qo + HD * 128], HD * 128)
        self.load_w(wz, self.w_in_d[l][:, zo:zo + HD * 128], HD * 128)
        for h in range(HD):
            ps = self.ps()
            for k in range(8):
                c.op("pe", lambda E: E.matmul(ps[:, 0:M], wkv[:, k, h * 128:(h + 1) * 128], memT[:, k, :], start=(k == 0), stop=(k == 7)), w=[ps], r=[wkv, memT])
            c.op("act", lambda E: E.copy(KT[:, h, :], ps[:, 0:M]), w=[KT], r=[ps])
        for j in range(M // 128):
            ps = self.ps()
            for k in range(8):
                c.op("pe", lambda E: E.matmul(ps[:, 0:HD * 128], memT[:, k, j * 128:(j + 1) * 128], wkv[:, k, HD * 128:2 * HD * 128], start=(k == 0), stop=(k == 7)), w=[ps], r=[wkv, memT])
            c.op("act", lambda E: E.copy(Vt[:, j, :], ps[:, 0:HD * 128]), w=[Vt], r=[ps])
        scale = 128.0 ** -0.5
        nj = M // 128
        units = [(nb, h) for nb in range(NB) for h in range(HD)]
        Eb6 = Eb + [c.sb(f"dE{i}", [128, 512], BF16) for i in range(4, 4 + 2 * nj - 4 + 4)] if False else Eb
        state = {}

        def s0(u):
            nb, h = units[u]
            q = qTb[u % 2]
            ps = self.ps()
            self.proj_fm(ps, wq, h * 128, 128, nb * 512, 512)
            c.op("act", lambda E: E.copy(q[:], ps[:]), w=[q], r=[ps])

        def s1(u):
            nb, h = units[u]
            q = qTb[u % 2]
            Es = []
            for j in range(nj):
                ps = self.ps()
                c.op("pe", lambda E: E.matmul(ps[:], KT[:, h, j * 128:(j + 1) * 128], q[:], start=True, stop=True), w=[ps], r=[KT, q])
                Ej = Eb[(u * nj + j) % 4]
                c.op("act", lambda E: E.activation(Ej[:], ps[:], AF.Exp, scale=scale), w=[Ej], r=[ps])
                Es.append(Ej)
            state[u] = Es

        def s2(u):
            nb, h = units[u]
            sz, rd, yy, ybb = szb[u % 2], rden[u % 2], y1[u % 2], yb[u % 2]
            Es = state.pop(u)
            ps = self.ps()
            self.proj_fm(ps, wz, h * 128, 128, nb * 512, 512)
            c.op("act", lambda E: E.activation(sz[:], ps[:], AF.Silu), w=[sz], r=[ps])
            po = self.ps()
            pd = self.ps()
            for j in range(nj):
                c.op("pe", lambda E: E.matmul(po[:], Vt[:, j, h * 128:(h + 1) * 128], Es[j][:], start=(j == 0), stop=(j == nj - 1)), w=[po], r=[Vt, Es[j]])
            for j in range(nj):
                c.op("pe", lambda E: E.matmul(pd[:], self.cst("ones", True), Es[j][:], start=(j == 0), stop=(j == nj - 1)), w=[pd], r=[self.cb, Es[j]])
            c.op("dve", lambda E: E.reciprocal(rd[:], pd[:]), w=[rd], r=[pd])
            c.op("dve", lambda E: E.tensor_tensor(yy[:], po[:], rd[:], ALU.mult), w=[yy], r=[po, rd])
            c.op("pool", lambda E: E.tensor_tensor(ybb[:], yy[:], sz[:], ALU.mult), w=[ybb], r=[yy, sz])
            r0 = cfg.yD + h * 128
            c.dma("sp", self.yT_d[r0:r0 + 128, nb * 512:(nb + 1) * 512], ybb[:], w=[self.yT_d], r=[ybb])

        pipeline(len(units), [s0, s1, s2])

    def phase_O(self, l, last):
        cfg, c = self.cfg, self.c
        T, NT, NY = cfg.T, cfg.NT, cfg.NY
        chunks = []
        for r in range(0, cfg.yB, 128):
            chunks.append((r, 128))
        for r in range(cfg.yB, cfg.yC, 64):
            chunks.append((r, 64))
        for r in range(cfg.yC, NY, 128):
            chunks.append((r, 128))
        nch = len(chunks)
        wo = c.sb("wo", [128, nch, D], BF16)
        wos = [c.sb(f"wos{i}", [128, D]) for i in range(2)]
        for i, (r, n) in enumerate(chunks):
            st = wos[i % 2]
            c.dma("sp", st[0:n, :], self.w_out_d[l][r:r + n, :], w=[st], r=[self.w_out_d])
            c.op("pool" if i % 2 else "dve", lambda E: E.tensor_copy(wo[0:n, i, :], st[0:n, :]), w=[wo], r=[st])
        yt = [c.sb(f"oyt{i}", [128, nch, 128], BF16) for i in range(2)]
        xo = [c.sb(f"oxo{i}", [128, D]) for i in range(2)]
        src, srcb, adds = self.xsrc
        if last and self.mode != "partial":
            self.load_grow(("fg", 0))
        nA = cfg.yB // 128
        nBc = (cfg.yC - cfg.yB) // 64
        nCD = (NY - cfg.yC) // 128
        nar = T // 1024
        li = self.layers.index(l)

        def o_load(t):
            y = yt[t % 2]
            if nA:
                c.dma("sp", y[:, 0:nA, :], self.yT_d[0:cfg.yB, t * 128:(t + 1) * 128].rearrange("(c p) n -> p c n", p=128), w=[y], r=[self.yT_d])
            if nBc:
                c.dma("sp", y[0:64, nA:nA + nBc, :], self.yT_d[cfg.yB:cfg.yC, t * 128:(t + 1) * 128].rearrange("(c p) n -> p c n", p=64), w=[y], r=[self.yT_d])
            if nCD:
                c.dma("sp", y[:, nA + nBc:nch, :], self.yT_d[cfg.yC:NY, t * 128:(t + 1) * 128].rearrange("(c p) n -> p c n", p=128), w=[y], r=[self.yT_d])
            if self.mode not in ("partial", "split"):
                xb = self.xt[t % 2]
                c.dma("sp", xb[:], src(t), w=[xb], r=srcb)
                if adds is not None:
                    for ai_, fn in enumerate(adds):
                        xa = self.xa[(t + ai_) % 2]
                        ap_, bufs = fn(t)
                        c.dma("sp", xa[:], ap_, w=[xa], r=bufs)
                        c.op("pool", lambda E: E.tensor_tensor(xb[:], xb[:], xa[:], ALU.add), w=[xb], r=[xb, xa])

        def o_comp(t):
            y, xb, xn = yt[t % 2], self.xt[t % 2], xo[t % 2]
            for half in range(2):
                ps = self.ps()
                for i, (r, n) in enumerate(chunks):
                    c.op("pe", lambda E: E.matmul(ps[:], y[0:n, i, :], wo[0:n, i, half * 512:(half + 1) * 512], start=(i == 0), stop=(i == nch - 1)), w=[ps], r=[y, wo])
                if self.mode in ("partial", "split"):
                    c.op("act", lambda E: E.copy(xn[:, half * 512:(half + 1) * 512], ps[:]), w=[xn], r=[ps])
                else:
                    c.op("dve", lambda E: E.tensor_tensor(xn[:, half * 512:(half + 1) * 512], ps[:], xb[:, half * 512:(half + 1) * 512], ALU.add), w=[xn], r=[ps, xb])
            if last and self.mode not in ("partial", "split"):
                self.final_norm(xn)

        def o_store(t):
            xn = xo[t % 2]
            if self.mode == "split":
                ch = li * nar + t // 8
                c.dma("sp", self.arin[ch][(t % 8) * 128:(t % 8 + 1) * 128, :], xn[:], w=[self.arin[ch]], r=[xn])
                if t % 8 == 7:
                    c.collective("AllReduce", ALU.add, self.groups, self.arin[ch], self.arout[ch])
            elif self.mode == "partial" or last:
                c.dma("sp", self.out_d[t * 128:(t + 1) * 128, :], xn[:], w=[self.out_d], r=[xn])
            else:
                c.dma("sp", self.x1_d[t * 128:(t + 1) * 128, :], xn[:], w=[self.x1_d], r=[xn])

        pipeline(NT, [o_load, o_comp, o_store])

        if self.mode == "split" and last:
            fns = list(adds or []) + [self.ar_src(li)]

            def f_load(t):
                xn = xo[t % 3]
                c.dma("sp", xn[:], src(t), w=[xn], r=srcb)
                for ai_, fn in enumerate(fns):
                    xa = self.xa3[(t * len(fns) + ai_) % 3]
                    ap_, bufs = fn(t)
                    c.dma("sp", xa[:], ap_, w=[xa], r=bufs)
                    c.op("pool", lambda E: E.tensor_tensor(xn[:], xn[:], xa[:], ALU.add), w=[xn], r=[xn, xa])

            def f_comp(t):
                self.final_norm(xo[t % 3])

            def f_store(t):
                xn = xo[t % 3]
                c.dma("sp", self.out_d[t * 128:(t + 1) * 128, :], xn[:], w=[self.out_d], r=[xn])

            self.xa3 = list(self.xa[0:1]) + [self.xt[0], self.xt[1]]
            xo.append(c.sb("oxo2", [128, D]))
            pipeline(NT, [f_load, f_comp, f_store])

    def final_norm(self, xn):
        c = self.c
        c.op("act", lambda E: E.activation(self.hbs[0][:], xn[:], AF.Square, accum_out=self.ss[:]), w=[self.hbs[0], self.ss], r=[xn])
        c.op("pool", lambda E: E.tensor_scalar(self.rstd[:], self.ss[:], 1.0 / D, EPS, ALU.mult, ALU.add), w=[self.rstd], r=[self.ss])
        c.op("pool", lambda E: E.tensor_tensor(self.rstd[:], self.rstd[:], self.negh[:, 0:1], ALU.pow), w=[self.rstd], r=[self.rstd, self.negh])
        c.op("dve", lambda E: E.scalar_tensor_tensor(xn[:], xn[:], self.rstd[:], self.grow[:], ALU.mult, ALU.mult), w=[xn], r=[xn, self.rstd, self.grow])


SIZES = (512, 512, 512, 512, 512, 512, 128, 128, 512, 256, 256, 512, 512, 512, 8, 8, 512, 512)
NAMES = ("a_q", "a_i", "a_ff", "a_fb", "a_z", "b_q", "b_k", "b_v", "b_z", "c_q", "c_k", "c_v", "c_o", "c_z", "c_ig", "c_fg", "d_q", "d_z")
OFFS = dict(zip(NAMES, np.concatenate([[0], np.cumsum(SIZES)[:-1]]).tolist()))


def pack_weights(cfg, inp, selA, selB, selC, selD):
    L = cfg.L
    w_in = inp["w_in"]
    cols = []
    for n in ("a_q", "a_i", "a_ff", "a_fb", "a_z"):
        cols += [OFFS[n] + h * 128 + np.arange(128) for h in selA]
    cols += [OFFS["b_q"] + (g * 4 + j) * 64 + np.arange(64) for g in selB for j in range(4)]
    cols += [OFFS["b_k"] + g * 64 + np.arange(64) for g in selB]
    cols += [OFFS["b_v"] + g * 64 + np.arange(64) for g in selB]
    cols += [OFFS["b_z"] + (g * 4 + j) * 64 + np.arange(64) for g in selB for j in range(4)]
    cols += [OFFS["c_q"] + h * 64 + np.arange(64) for h in selC]
    cols += [OFFS["c_k"] + h * 64 + np.arange(64) for h in selC]
    for n in ("c_v", "c_o", "c_z"):
        cols += [OFFS[n] + h * 128 + np.arange(128) for h in selC]
    for p in range(len(selC) // 2):
        hh = selC[2 * p:2 * p + 2]
        cols += [np.array([OFFS["c_ig"] + h for h in hh] + [OFFS["c_ig"] + 4 + h for h in hh]
                          + [OFFS["c_fg"] + h for h in hh] + [OFFS["c_fg"] + 4 + h for h in hh])]
    for n in ("d_q", "d_z"):
        cols += [OFFS[n] + h * 128 + np.arange(128) for h in selD]
    cols = np.concatenate(cols)
    assert len(cols) == cfg.NIN, (len(cols), cfg.NIN)
    w_in_c = np.ascontiguousarray(w_in[:, :, cols])
    kvc = np.concatenate([h * 128 + np.arange(128) for h in selD] + [512 + h * 128 + np.arange(128) for h in selD])
    w_kv_c = np.ascontiguousarray(inp["w_mem_kv"][:, :, kvc])
    rows = np.concatenate([h * 128 + np.arange(128) for h in selA] + [512 + (g * 4 + j) * 64 + np.arange(64) for g in selB for j in range(4)]
                          + [1024 + h * 128 + np.arange(128) for h in selC] + [1536 + h * 128 + np.arange(128) for h in selD])
    w_out_c = np.ascontiguousarray(inp["w_out"][:, rows, :])
    pp = np.zeros((128, max(cfg.NPP, 1)), np.float32)
    for l in range(L):
        for i, h in enumerate(selA):
            o = cfg.pp[("A", l, i)]
            pp[:, o] = inp["hgrn_lb_logits"][0, h * 128:(h + 1) * 128]
            pp[:, o + 1] = inp["hgrn_lb_logits"][l, h * 128:(h + 1) * 128]
            pp[:, o + 2] = inp["hgrn_norm_g"][l, h * 128:(h + 1) * 128]
        for p in range(len(selC) // 2):
            o = cfg.pp[("Cw", l, p)]
            ch = np.concatenate([selC[2 * p] * 64 + np.arange(64), selC[2 * p + 1] * 64 + np.arange(64)])
            for j in range(5):
                pp[:, o + j] = inp["mlstm_conv_w"][l, j, ch]
                pp[:, o + 5 + j] = inp["mlstm_conv_w"][l, j, 256 + ch]
    rp = np.zeros((1, cfg.NRP), np.float32)
    for l in range(L):
        o, w = cfg.rp[("ng", l)]
        rp[0, o:o + w] = inp["norm_g"][l]
        o, w = cfg.rp[("mg", l)]
        rp[0, o:o + w] = inp["mem_norm_g"][l]
        o, w = cfg.rp[("gC", l)]
        rp[0, o:o + w] = np.concatenate([inp["mlstm_norm_g"][l, h * 128:(h + 1) * 128] for h in selC])
        o, w = cfg.rp[("gb", l)]
        gb = inp["mlstm_gate_b"][l]
        rp[0, o:o + w] = np.concatenate([[gb[h] for h in selC[2 * p:2 * p + 2]] + [gb[4 + h] for h in selC[2 * p:2 * p + 2]]
                                         + [gb[8 + h] for h in selC[2 * p:2 * p + 2]] + [gb[12 + h] for h in selC[2 * p:2 * p + 2]]
                                         for p in range(len(selC) // 2)])
        o, w = cfg.rp[("sink", l)]
        rp[0, o:o + w] = np.concatenate([inp["attn_sink"][l, g * 4:(g + 1) * 4] for g in selB])
    o, w = cfg.rp[("fg", 0)]
    rp[0, o:o + w] = inp["final_norm_g"]
    return {"w_in": w_in_c, "w_kv": w_kv_c, "w_out": w_out_c, "pp": pp, "rp": rp, "cc": make_consts(cfg)}


def _bc(ap, axis, n):
    a = ap.unsqueeze(axis)
    shp = list(a.shape)
    shp[axis] = n
    return a.broadcast_to(shp)


def rope_tables(self):
    cfg, c = self.cfg, self.c
    T = cfg.T
    self.cs_d = c.dram("cs_s", [2, 64, T], F32)
    m = self.mark()
    CH = min(T, 1024)
    posi = c.sb("posi", [16, CH], I32)
    tt = c.sb("rtt", [16, CH])
    kf = c.sb("rkf", [16, CH])
    ki = c.sb("rki", [16, CH], I32)
    mm = c.sb("rmm", [16, CH])
    C64 = c.sb("rC64", [64, CH])
    S64 = c.sb("rS64", [64, CH])
    sc = 2 * math.pi * (1 - 2e-6)
    for c0 in range(0, T, CH):
        c.dma("sp", posi[:], self.pos_d[0:1, c0:c0 + CH].partition_broadcast(16), w=[posi], r=[self.pos_d])
        c.op("dve", lambda E: E.tensor_copy(tt[:], posi[:]), w=[tt], r=[posi])
        c.op("dve", lambda E: E.tensor_scalar(tt[:], tt[:], self.cst("invf", rows=16), None, ALU.mult), w=[tt], r=[tt, self.cf])
        c.op("dve", lambda E: E.tensor_copy(ki[:], tt[:]), w=[ki], r=[tt])
        c.op("dve", lambda E: E.tensor_copy(kf[:], ki[:]), w=[kf], r=[ki])
        c.op("dve", lambda E: E.tensor_tensor(tt[:], tt[:], kf[:], ALU.subtract), w=[tt], r=[tt, kf])
        c.op("pool", lambda E: E.memset(C64[:], 1.0), w=[C64])
        c.op("pool", lambda E: E.memset(S64[:], 0.0), w=[S64])
        c.op("dve", lambda E: E.tensor_single_scalar(mm[:], tt[:], 0.5, ALU.is_gt), w=[mm], r=[tt])
        c.op("dve", lambda E: E.tensor_tensor(kf[:], tt[:], mm[:], ALU.subtract), w=[kf], r=[tt, mm])
        c.op("act", lambda E: E.activation(S64[0:16, :], kf[:], AF.Sin, scale=sc), w=[S64], r=[kf])
        c.op("dve", lambda E: E.tensor_scalar_add(tt[:], tt[:], 0.25), w=[tt], r=[tt])
        c.op("dve", lambda E: E.tensor_single_scalar(mm[:], tt[:], 0.5, ALU.is_gt), w=[mm], r=[tt])
        c.op("dve", lambda E: E.tensor_tensor(tt[:], tt[:], mm[:], ALU.subtract), w=[tt], r=[tt, mm])
        c.op("dve", lambda E: E.tensor_single_scalar(mm[:], tt[:], 0.5, ALU.is_gt), w=[mm], r=[tt])
        c.op("dve", lambda E: E.tensor_tensor(kf[:], tt[:], mm[:], ALU.subtract), w=[kf], r=[tt, mm])
        c.op("act", lambda E: E.activation(C64[0:16, :], kf[:], AF.Sin, scale=sc), w=[C64], r=[kf])
        c.dma("sp", self.cs_d[0][:, c0:c0 + CH], C64[:], w=[self.cs_d], r=[C64])
        c.dma("sp", self.cs_d[1][:, c0:c0 + CH], S64[:], w=[self.cs_d], r=[S64])
    self.release(m)


def phase_B(self, l):
    cfg, c = self.cfg, self.c
    T, NT, NB, GB = cfg.T, cfg.NT, cfg.NB, cfg.GB
    wq = c.sb("wbq", [128, 8, GB * 256], BF16)
    wk = c.sb("wbk", [128, 8, GB * 64], BF16)
    wv = c.sb("wbv", [128, 8, GB * 64], BF16)
    wz = c.sb("wbz", [128, 8, GB * 256], BF16)
    for (wb, n) in ((wq, "B_q"), (wk, "B_k"), (wv, "B_v"), (wz, "B_z")):
        o, w = cfg.col[n]
        self.load_w(wb, self.w_in_d[l][:, o:o + w], w)
    kT = c.sb("bkT", [64, GB, T], BF16)
    v_tm = c.sb("bvtm", [128, NT, GB * 64], BF16)
    sinkexp = c.sb("bsink", [64, GB * 4])
    cs = [c.sb(f"bcs{i}", [64, 2, 512]) for i in range(2)]
    xf = [c.sb(f"bxf{i}", [64, 512]) for i in range(2)]
    t1 = [c.sb(f"bt1{i}", [64, 512]) for i in range(2)]
    t2 = [c.sb(f"bt2{i}", [64, 512]) for i in range(2)]
    qT = [c.sb(f"bqT{i}", [64, 4, 512], BF16) for i in range(2)]
    szB = [c.sb(f"bsz{i}", [64, 4, 512], BF16) for i in range(2)]
    Eb = [c.sb(f"bE{i}", [128, 4, 128], BF16) for i in range(6)]
    den = [c.sb(f"bden{i}", [64, 4, 128]) for i in range(2)]
    y1 = [c.sb(f"by1{i}", [64, 4, 128]) for i in range(2)]
    yb = [c.sb(f"byb{i}", [64, 4, 128], BF16) for i in range(2)]
    so, sw = cfg.rp[("sink", l)]
    c.op("act", lambda E: E.activation(sinkexp[:], self.rowt[0:64, so:so + sw], AF.Exp), w=[sinkexp], r=[self.rowt])
    P64 = self.cf[0:64, cfg.cc["P64T"][0]:cfg.cc["P64T"][0] + 64]
    self._ri = 0

    def rope(ps, cst, dst_ap, dst_buf):
        i = self._ri
        self._ri += 1
        x, a, b = xf[i % 2], t1[i % 2], t2[i % 2]
        c.op("act", lambda E: E.copy(x[:], ps[0:64, :]), w=[x], r=[ps])
        pr = self.ps()
        c.op("pe", lambda E: E.matmul(pr[0:64, :], P64, x[:], start=True, stop=True), w=[pr], r=[self.cf, x])
        c.op("pool", lambda E: E.tensor_tensor(a[:], x[:], cst[:, 0, :], ALU.mult), w=[a], r=[x, cst])
        c.op("dve", lambda E: E.tensor_tensor(b[:], pr[0:64, :], cst[:, 1, :], ALU.mult), w=[b], r=[pr, cst])
        c.op("pool", lambda E: E.tensor_tensor(dst_ap, a[:], b[:], ALU.add), w=[dst_buf], r=[a, b])

    def load_cs(nb):
        t = cs[nb % 2]
        c.dma("sp", t[:], self.cs_d[:, :, nb * 512:(nb + 1) * 512].rearrange("a p n -> p a n"), w=[t], r=[self.cs_d])
        return t

    for nb in range(NB):
        cst = load_cs(nb)
        for g in range(GB):
            ps = self.ps()
            self.proj_fm(ps, wk, g * 64, 64, nb * 512, 512)
            rope(ps, cst, kT[:, g, nb * 512:(nb + 1) * 512], kT)
        for tt in range(4):
            tile = nb * 4 + tt
            ps = self.ps()
            self.proj_tm(ps, wv, 0, GB * 64, tile)
            c.op("act", lambda E: E.copy(v_tm[:, tile, :], ps[:, 0:GB * 64]), w=[v_tm], r=[ps])
    self.tm("B.pass1_done")
    onesb = self.cb[:, cfg.cc["ones"][0]:cfg.cc["ones"][0] + 64]
    mL = self.cb[:, cfg.cc["triL"][0]:cfg.cc["triL"][0] + 128]
    mU = self.cb[:, cfg.cc["triU"][0]:cfg.cc["triU"][0] + 128]
    Eb9 = Eb + [c.sb(f"bE{i}", [128, 4, 128], BF16) for i in range(6, 9)]
    units = [(nb, g, qt) for nb in range(NB) for g in range(GB) for qt in range(4)]
    state = {}

    def proj(nb, g):
        bi = (nb * GB + g) % 2
        q, sz = qT[bi], szB[bi]
        cst = load_cs(nb)
        for j in range(4):
            ps = self.ps()
            self.proj_fm(ps, wq, (g * 4 + j) * 64, 64, nb * 512, 512)
            rope(ps, cst, q[:, j, :], q)
            ps = self.ps()
            self.proj_fm(ps, wz, (g * 4 + j) * 64, 64, nb * 512, 512)
            c.op("act", lambda E: E.activation(sz[:, j, :], ps[0:64, :], AF.Silu), w=[sz], r=[ps])

    def sA(u):
        nb, g, qt = units[u]
        q = qT[(nb * GB + g) % 2]
        Tq = nb * 4 + qt
        kts = [k for k in (Tq - 1, Tq, Tq + 1) if 0 <= k < NT]
        Es = []
        for i, kt in enumerate(kts):
            ps = self.ps()
            c.op("pe", lambda E: E.matmul(ps[:].rearrange("p (j n) -> p j n", j=4), kT[:, g, kt * 128:(kt + 1) * 128], q[:, :, qt * 128:(qt + 1) * 128], start=True, stop=True), w=[ps], r=[kT, q])
            Ek = Eb9[(u * 3 + i) % 9]
            c.op("act", lambda E: E.activation(Ek[:], ps[:].rearrange("p (j n) -> p j n", j=4), AF.Exp, scale=0.125), w=[Ek], r=[ps])
            if kt != Tq:
                mk = mL if kt < Tq else mU
                c.op("pool", lambda E: E.tensor_tensor(Ek[:], Ek[:], _bc(mk, 1, 4), ALU.mult), w=[Ek], r=[Ek, self.cb])
            Es.append(Ek)
        state[u] = (kts, Es)

    def sB(u):
        nb, g, qt = units[u]
        sz = szB[(nb * GB + g) % 2]
        Tq = nb * 4 + qt
        kts, Es = state.pop(u)
        po = self.ps()
        pd = self.ps()
        n = len(kts)
        for i, kt in enumerate(kts):
            c.op("pe", lambda E: E.matmul(po[0:64, :], v_tm[:, kt, g * 64:(g + 1) * 64], Es[i][:].rearrange("p j n -> p (j n)"), start=(i == 0), stop=(i == n - 1)), w=[po], r=[v_tm, Es[i]])
        for i, kt in enumerate(kts):
            c.op("pe", lambda E: E.matmul(pd[0:64, :], onesb, Es[i][:].rearrange("p j n -> p (j n)"), start=(i == 0), stop=(i == n - 1)), w=[pd], r=[self.cb, Es[i]])
        dn, yy, ybb = den[u % 2], y1[u % 2], yb[u % 2]
        c.op("dve", lambda E: E.tensor_tensor(dn[:], pd[0:64, :].rearrange("p (j n) -> p j n", j=4), _bc(sinkexp[:, g * 4:(g + 1) * 4], 2, 128), ALU.add), w=[dn], r=[pd, sinkexp])
        c.op("dve", lambda E: E.reciprocal(dn[:], dn[:]), w=[dn], r=[dn])
        c.op("dve", lambda E: E.tensor_tensor(yy[:], po[0:64, :].rearrange("p (j n) -> p j n", j=4), dn[:], ALU.mult), w=[yy], r=[po, dn])
        c.op("pool", lambda E: E.tensor_tensor(ybb[:], yy[:], sz[:, :, qt * 128:(qt + 1) * 128], ALU.mult), w=[ybb], r=[yy, sz])
        r0 = cfg.yB + g * 256
        c.dma("sp", self.yT_d[r0:r0 + 256, Tq * 128:(Tq + 1) * 128].rearrange("(j p) n -> p j n", p=64), ybb[:], w=[self.yT_d], r=[ybb])

    blocks = [(nb, g) for nb in range(NB) for g in range(GB)]
    proj(*blocks[0])
    if len(blocks) > 1:
        proj(*blocks[1])
    nu = len(units)
    for u in range(nu + 2):
        if 0 <= u - 2 < nu:
            sB(u - 2)
        if u % 4 == 2 and u // 4 + 1 < len(blocks) and u // 4 >= 1:
            proj(*blocks[u // 4 + 1])
        if u < nu:
            sA(u)


Gen.rope_tables = rope_tables
Gen.phase_B = phase_B


def phase_A(self, l):
    for h in range(self.cfg.HA):
        m = self.mark()
        self.phase_A_head(l, h)
        self.release(m)


def phase_A_head(self, l, h):
    cfg, c = self.cfg, self.c
    T, NT, NB = cfg.T, cfg.NT, cfg.NB
    NCH = T // 64
    qt = {d: c.sb(f"aqt{d}", [128, T], BF16) for d in "fb"}
    kt = {d: c.sb(f"akt{d}", [128, T], BF16) for d in "fb"}
    ktm = {d: c.sb(f"aktm{d}", [128, NT, 128], BF16) for d in "fb"}
    ebl = {d: c.sb(f"aebl{d}", [128, NCH]) for d in "fb"}
    v_tm = c.sb("avtm", [128, NT, 128], BF16)
    wz = c.sb("awz", [128, 8, 128], BF16)
    lbt = c.sb("albt", [128, 1])
    om = c.sb("aom", [128, 1])
    scanm = self.cst("scanm")
    ident = self.cst("ident", True)
    onesb = self.cst("ones", True)
    mask = {"f": self.cst("bdU"), "b": self.cst("bdL")}
    zo_, _ = cfg.col["A_z"]
    self.load_w_into(wz, 0, self.w_in_d[l][:, zo_ + h * 128:zo_ + (h + 1) * 128], 128)
    po_ = cfg.pp[("A", l, h)]
    if l == 0:
        c.op("pool", lambda E: E.memset(lbt[:], 0.0), w=[lbt])
        c.op("pool", lambda E: E.memset(om[:], 1.0), w=[om])
    else:
        c.op("dve", lambda E: E.tensor_tensor(lbt[:], self.ppt[:, po_ + 1:po_ + 2], self.ppt[:, po_:po_ + 1], ALU.subtract), w=[lbt], r=[self.ppt])
        c.op("act", lambda E: E.activation(lbt[:], lbt[:], AF.Sigmoid), w=[lbt], r=[lbt])
        c.op("dve", lambda E: E.tensor_scalar(om[:], lbt[:], -1.0, 1.0, ALU.mult, ALU.add), w=[om], r=[lbt])
    gA = self.ppt[:, po_ + 2:po_ + 3]
    mpre = self.mark()
    w4 = c.sb("aw4", [128, 8, 4 * 128], BF16)
    for j, n in enumerate(("A_q", "A_i", "A_ff", "A_fb")):
        o, _ = cfg.col[n]
        self.load_w_into(w4, j * 128, self.w_in_d[l][:, o + h * 128:o + (h + 1) * 128], 128)
    qf = [c.sb(f"aqf{i}", [128, 512]) for i in range(2)]
    tA = [c.sb(f"atA{i}", [128, 512]) for i in range(2)]
    tG = [c.sb(f"atG{i}", [128, 512]) for i in range(2)]
    tK = [c.sb(f"atK{i}", [128, 512]) for i in range(2)]
    tB = [c.sb(f"atB{i}", [128, 512]) for i in range(2)]
    tE1 = [c.sb(f"atE1{i}", [128, 512]) for i in range(2)]
    tE2 = [c.sb(f"atE2{i}", [128, 512]) for i in range(2)]
    tK.append(c.sb("atK2", [128, 512]))

    def pre0(u):
        nb, di = u // 2, u % 2
        a, g, k = tA[u % 2], tG[u % 2], tK[u % 3]
        if di == 0:
            q = qf[nb % 2]
            ps = self.ps()
            self.proj_fm(ps, w4, 0, 128, nb * 512, 512)
            c.op("act", lambda E: E.copy(q[:], ps[:]), w=[q], r=[ps])
            for tt in range(4):
                tile = nb * 4 + tt
                ps = self.ps()
                self.proj_tm(ps, w4, 128, 128, tile)
                c.op("act", lambda E: E.copy(v_tm[:, tile, :], ps[:, 0:128]), w=[v_tm], r=[ps])
        ps = self.ps()
        self.proj_fm(ps, w4, (2 + di) * 128, 128, nb * 512, 512)
        c.op("act", lambda E: E.activation(a[:], ps[:], AF.Sigmoid), w=[a], r=[ps])
        c.op("dve", lambda E: E.tensor_scalar(a[:], a[:], om[:], lbt[:], ALU.mult, ALU.add), w=[a], r=[a, om, lbt])
        c.op("act", lambda E: E.activation(g[:], a[:], AF.Ln), w=[g], r=[a])
        c.op("pool", lambda E: E.tensor_scalar(k[:], a[:], -1.0, 1.0, ALU.mult, ALU.add), w=[k], r=[a])

    def pre1(u):
        nb, di = u // 2, u % 2
        d = "fb"[di]
        a, g, b, e1, e2 = tA[u % 2], tG[u % 2], tB[u % 2], tE1[u % 2], tE2[u % 2]
        c.op("dve", lambda E: E.tensor_tensor_scan(b[:], scanm, g[:], 0.0, ALU.mult, ALU.add), w=[b], r=[g, self.cf])
        b3 = b[:].rearrange("p (c n) -> p c n", n=64)
        if d == "f":
            c.op("pool", lambda E: E.tensor_tensor(a[:].rearrange("p (c n) -> p c n", n=64), b3, b3[:, :, 63:64].broadcast_to([128, 8, 64]), ALU.subtract), w=[a], r=[b])
        else:
            c.op("pool", lambda E: E.tensor_tensor(a[:], g[:], b[:], ALU.subtract), w=[a], r=[g, b])
        c.op("act", lambda E: E.activation(ebl[d][:, nb * 8:(nb + 1) * 8], b3[:, :, 63], AF.Exp), w=[ebl[d]], r=[b])
        c.op("act", lambda E: E.activation(e2[:], a[:], AF.Exp), w=[e2], r=[a])
        c.op("act", lambda E: E.activation(e1[:], a[:], AF.Exp, scale=-1.0), w=[e1], r=[a])

    def pre2(u):
        nb, di = u // 2, u % 2
        d = "fb"[di]
        blk = slice(nb * 512, (nb + 1) * 512)
        q, k, e1, e2 = qf[nb % 2], tK[u % 3], tE1[u % 2], tE2[u % 2]
        c.op("dve", lambda E: E.tensor_tensor(qt[d][:, blk], q[:], e2[:], ALU.mult), w=[qt[d]], r=[q, e2])
        c.op("pool", lambda E: E.tensor_tensor(kt[d][:, blk], k[:], e1[:], ALU.mult), w=[kt[d]], r=[k, e1])
        pT = self.pst()
        for tt in range(4):
            c.op("pe", lambda E: E.transpose(pT[:, tt * 128:(tt + 1) * 128], kt[d][:, nb * 512 + tt * 128:nb * 512 + (tt + 1) * 128], ident), w=[pT], r=[kt[d], self.cb])
        c.op("act", lambda E: E.copy(ktm[d][:, nb * 4:(nb + 1) * 4, :], pT[:, 0:512].rearrange("p (t n) -> p t n", t=4)), w=[ktm[d]], r=[pT])

    pipeline(2 * NB, [pre0, pre1, pre2])
    self.release(mpre)
    self.tm("A.pre_done")
    Ssn = c.sb("aSsn", [128, 2, NCH, 128], BF16)
    NSB = 3
    m_s = self.mark()
    S = {d: [c.sb(f"aS{d}{i}", [128, 128]) for i in range(NSB)] for d in "fb"}
    for i in range(NCH):
        for di, (d, ch) in enumerate((("f", i), ("b", NCH - 1 - i))):
            tile, cc = ch // 2, ch % 2
            rows = slice(cc * 64, (cc + 1) * 64)
            s_old, s_new = S[d][(i - 1) % NSB], S[d][i % NSB]
            pu = self.ps()
            c.op("pe", lambda E: E.matmul(pu[:, 0:128], ktm[d][rows, tile, :], v_tm[rows, tile, :], start=True, stop=True), w=[pu], r=[ktm[d], v_tm])
            if i == 0:
                c.op("pool", lambda E: E.memset(Ssn[:, di, ch, :], 0.0), w=[Ssn])
                c.op("dve", lambda E: E.tensor_copy(s_new[:], pu[:, 0:128]), w=[s_new], r=[pu])
            else:
                c.op("act", lambda E: E.activation(Ssn[:, di, ch, :], s_old[:], AF.Copy, scale=ebl[d][:, ch:ch + 1]), w=[Ssn], r=[s_old, ebl[d]])
                c.op("dve", lambda E: E.scalar_tensor_tensor(s_new[:], s_old[:], ebl[d][:, ch:ch + 1], pu[:, 0:128], ALU.mult, ALU.add), w=[s_new], r=[s_old, ebl[d], pu])
    self.release(m_s)
    self.tm("A.pass1_done")
    AT = {d: [c.sb(f"aAT{d}{i}", [128, 128], BF16) for i in range(3)] for d in "fb"}
    osum = [c.sb(f"aos{i}", [128, 512]) for i in range(2)]
    sq = [c.sb(f"asq{i}", [128, 512], BF16) for i in range(1)] * 2
    rsb = [c.sb(f"ars{i}", [128, 512]) for i in range(1)] * 2
    yyb = rsb
    szb = [c.sb(f"asz{i}", [128, 512]) for i in range(1)] * 2
    yb = [c.sb(f"ayb{i}", [128, 512], BF16) for i in range(1)] * 2
    def o0(tile):
        tk = slice(tile * 128, (tile + 1) * 128)
        for d in "fb":
            pa = self.ps()
            c.op("pe", lambda E: E.matmul(pa[:, 0:128], kt[d][:, tk], qt[d][:, tk], start=True, stop=True), w=[pa], r=[kt[d], qt[d]])
            at = AT[d][tile % 3]
            c.op("dve", lambda E: E.tensor_tensor(at[:], pa[:, 0:128], mask[d], ALU.mult), w=[at], r=[pa, self.cf])

    def o1(tile):
        nb, tt = tile // 4, tile % 4
        ob = osum[nb % 2]
        po = self.ps()
        for cc in range(2):
            ch = tile * 2 + cc
            rows = slice(cc * 64, (cc + 1) * 64)
            for di, d in enumerate("fb"):
                at = AT[d][tile % 3]
                c.op("pe", lambda E: E.matmul(po[:, rows], v_tm[rows, tile, :], at[rows, rows], start=(di == 0), stop=False), w=[po], r=[v_tm, at])
                c.op("pe", lambda E: E.matmul(po[:, rows], Ssn[:, di, ch, :], qt[d][:, tile * 128 + cc * 64:tile * 128 + (cc + 1) * 64], start=False, stop=(di == 1)), w=[po], r=[Ssn, qt[d]])
        c.op("act", lambda E: E.copy(ob[:, tt * 128:(tt + 1) * 128], po[:, 0:128]), w=[ob], r=[po])
        if tt != 3:
            return
        blk = slice(nb * 512, (nb + 1) * 512)
        s2, rs, yy, sz, ybb = sq[nb % 2], rsb[nb % 2], yyb[nb % 2], szb[nb % 2], yb[nb % 2]
        c.op("pool", lambda E: E.tensor_tensor(s2[:], ob[:], ob[:], ALU.mult), w=[s2], r=[ob])
        pss = self.ps()
        c.op("pe", lambda E: E.matmul(pss[:], onesb, s2[:], start=True, stop=True), w=[pss], r=[self.cb, s2])
        c.op("act", lambda E: E.activation(rs[:], pss[:], AF.Sqrt, bias=self.epsb[:], scale=1.0 / 128), w=[rs], r=[pss, self.epsb])
        c.op("dve", lambda E: E.reciprocal(rs[:], rs[:]), w=[rs], r=[rs])
        c.op("dve", lambda E: E.scalar_tensor_tensor(yy[:], ob[:], gA, rs[:], ALU.mult, ALU.mult), w=[yy], r=[ob, self.ppt, rs])
        ps = self.ps()
        self.proj_fm(ps, wz, 0, 128, nb * 512, 512)
        c.op("act", lambda E: E.activation(sz[:], ps[:], AF.Silu), w=[sz], r=[ps])
        c.op("pool", lambda E: E.tensor_tensor(ybb[:], yy[:], sz[:], ALU.mult), w=[ybb], r=[yy, sz])
        r0 = cfg.yA + h * 128
        c.dma("sp", self.yT_d[r0:r0 + 128, blk], ybb[:], w=[self.yT_d], r=[ybb])

    for step in range(NT + 2):
        if 0 <= step - 2 < NT:
            o1(step - 2)
        if step < NT:
            o0(step)


def load_w_into(self, dst, c0, src_ap, ncols):
    c = self.c
    for o in range(0, ncols, 128):
        n = min(128, ncols - o)
        st = self.wst[self.wsti % 2]
        e = ("dve", "pool")[self.wsti % 2]
        self.wsti += 1
        c.dma("sp", st[:, :, 0:n], src_ap[:, o:o + n].rearrange("(c p) n -> p c n", p=128), w=[st], r=[self.w_in_d, self.w_kv_d])
        c.op(e, lambda E: E.tensor_copy(dst[:, :, c0 + o:c0 + o + n], st[:, :, 0:n]), w=[dst], r=[st])


Gen.phase_A = phase_A
Gen.phase_A_head = phase_A_head
Gen.load_w_into = load_w_into


def phase_C(self, l):
    for p in range(self.cfg.HC // 2):
        m = self.mark()
        self.phase_C_pair(l, p)
        self.release(m)


def phase_C_pair(self, l, P):
    cfg, c = self.cfg, self.c
    T, NT, NB, HC = cfg.T, cfg.NT, cfg.NB, 2
    G2 = 4
    wq = c.sb("cwq", [128, 8, 128], BF16)
    wk = c.sb("cwk", [128, 8, 128], BF16)
    wv = c.sb("cwv", [128, 8, 256], BF16)
    wz2 = c.sb("cwz2", [128, 8, 256], BF16)
    wg = c.sb("cwg", [128, 8, 8], BF16)
    for (wb, n, w) in ((wq, "C_q", 128), (wk, "C_k", 128), (wv, "C_v", 256), (wg, "C_g", 8)):
        o, _ = cfg.col[n]
        self.load_w(wb, self.w_in_d[l][:, o + P * w:o + (P + 1) * w], w)
    qc = c.sb("cqc", [128, T], BF16)
    kc = c.sb("ckc", [128, T], BF16)
    ktm = c.sb("cktm", [128, NT, 2, 128], BF16)
    vp = c.sb("cvp", [128, NT, 2, 130], BF16)
    eb8 = c.sb("ceb8", [128, NT, G2])
    ew = c.sb("cew", [128, NT, G2])
    ebl = c.sb("cebl", [128, NT, G2])
    eblp = c.sb("ceblp", [128, NT, 2])
    Cb = c.sb("cCb", [128, 2, NT, 130], BF16)
    ident = self.cst("ident", True)
    bo, _ = cfg.rp[("gb", l)]
    bo, bw = bo + P * 8, 8
    m_conv = self.mark()
    raw = c.sb("craw", [128, T + 4])
    acc = [c.sb(f"cacc{i}", [128, 512]) for i in range(2)]
    c.op("pool", lambda E: E.memset(raw[:, 0:2], 0.0), w=[raw])
    c.op("pool", lambda E: E.memset(raw[:, T + 2:T + 4], 0.0), w=[raw])
    ai = 0
    for which, (wb, dst) in enumerate(((wq, qc), (wk, kc))):
        for nb in range(NB):
            ps = self.ps()
            self.proj_fm(ps, wb, 0, 128, nb * 512, 512)
            c.op("act", lambda E: E.copy(raw[:, 2 + nb * 512:2 + (nb + 1) * 512], ps[:]), w=[raw], r=[ps])
        wo_ = cfg.pp[("Cw", l, P)] + 5 * which
        for nb in range(NB):
            a = acc[ai % 2]
            ai += 1
            c.op("pool", lambda E: E.tensor_scalar(a[:], raw[:, nb * 512:nb * 512 + 512], self.ppt[:, wo_:wo_ + 1], None, ALU.mult), w=[a], r=[raw, self.ppt])
            for j in range(1, 5):
                c.op("dve", lambda E: E.scalar_tensor_tensor(a[:], raw[:, nb * 512 + j:nb * 512 + j + 512], self.ppt[:, wo_ + j:wo_ + j + 1], a[:], ALU.mult, ALU.add), w=[a], r=[raw, self.ppt, a])
            c.op("act", lambda E: E.activation(dst[:, nb * 512:(nb + 1) * 512], a[:], AF.Silu), w=[dst], r=[a])
    self.release(m_conv)
    self.tm("C.conv_done")
    c.op("pool", lambda E: E.memset(ktm[:], 0.0), w=[ktm])
    for t0 in range(0, NT, 8):
        n = min(8, NT - t0)
        pT = self.pst()
        for tt in range(n):
            c.op("pe", lambda E: E.transpose(pT[:, tt * 128:(tt + 1) * 128], kc[:, (t0 + tt) * 128:(t0 + tt + 1) * 128], ident), w=[pT], r=[kc, self.cb])
        pv = pT[:, 0:n * 128].rearrange("p (t n) -> p t n", t=n)
        c.op("act", lambda E: E.copy(ktm[:, t0:t0 + n, 0, 0:64], pv[:, :, 0:64]), w=[ktm], r=[pT])
        c.op("dve", lambda E: E.tensor_copy(ktm[:, t0:t0 + n, 1, 64:128], pv[:, :, 64:128]), w=[ktm], r=[pT])
    c.op("pool", lambda E: E.memset(vp[:, :, :, 128:130], 1.0), w=[vp])
    triU, triL, onesf = self.cst("triU"), self.cst("triL"), self.cst("ones")
    gall = c.sb("cgall", [128, NT, 8])
    fall = c.sb("cfall", [128, NT, G2])
    tgall = c.sb("ctgall", [128, NT, G2])
    for tile in range(NT):
        ps = self.ps()
        self.proj_tm(ps, wv, 0, 256, tile)
        c.op("act", lambda E: E.copy(vp[:, tile, :, 0:128], ps[:, 0:256].rearrange("p (h n) -> p h n", h=2)), w=[vp], r=[ps])
        ps = self.ps()
        self.proj_tm(ps, wg, 0, 8, tile)
        c.op("dve", lambda E: E.tensor_tensor(gall[:, tile, :], ps[:, 0:8], self.rowt[:, bo:bo + bw], ALU.add), w=[gall], r=[ps, self.rowt])
    c.op("act", lambda E: E.activation(fall[:], gall[:, :, G2:2 * G2], AF.Sigmoid), w=[fall], r=[gall])
    c.op("act", lambda E: E.activation(fall[:], fall[:], AF.Ln), w=[fall], r=[fall])
    pc = self.ps()
    pc3 = pc[:, 0:NT * 8].rearrange("p (t g) -> p t g", g=8)
    for tile in range(NT):
        c.op("pe", lambda E: E.matmul(pc[:, tile * 8:tile * 8 + 2], triU, fall[:, tile, 0:2], start=True, stop=True), w=[pc], r=[self.cf, fall])
        c.op("pe", lambda E: E.matmul(pc[:, tile * 8 + 2:tile * 8 + 4], triL, fall[:, tile, 2:4], start=True, stop=True), w=[pc], r=[self.cf, fall])
        c.op("pe", lambda E: E.matmul(pc[:, tile * 8 + 4:tile * 8 + 8], onesf, fall[:, tile, 0:4], start=True, stop=True), w=[pc], r=[self.cf, fall])
    c.op("act", lambda E: E.activation(tgall[:], pc3[:, :, 0:G2], AF.Exp), w=[tgall], r=[pc])
    c.op("pool", lambda E: E.tensor_scalar(eb8[:], tgall[:], 0.125, None, ALU.mult), w=[eb8], r=[tgall])
    c.op("dve", lambda E: E.tensor_tensor(tgall[:], gall[:, :, 0:G2], pc3[:, :, 0:G2], ALU.subtract), w=[tgall], r=[gall, pc, tgall])
    c.op("act", lambda E: E.activation(ew[:], tgall[:], AF.Exp), w=[ew], r=[tgall])
    c.op("act", lambda E: E.activation(ebl[:], pc3[:, :, G2:2 * G2], AF.Exp), w=[ebl], r=[pc])
    import os
    CSTOP = int(os.environ.get("CSTOP", 9))
    if CSTOP <= 1:
        return
    for hh in range(2):
        rows = slice(hh * 64, hh * 64 + 64)
        c.op("pool", lambda E: E.tensor_copy(eblp[rows, :, :], ebl[rows, :, hh:4:2]), w=[eblp], r=[ebl])

    def mk_v2(dst, tile):
        c.op("pool", lambda E: E.tensor_tensor(dst[:], vp[:, tile].unsqueeze(1).broadcast_to([128, 2, 2, 130]),
                                                ew[:, tile, :].rearrange("p (d h) -> p d h", d=2).unsqueeze(3).broadcast_to([128, 2, 2, 130]), ALU.mult),
             w=[dst], r=[vp, ew])

    if CSTOP <= 2:
        return
    self.tm("C.gates_done")
    V2 = [c.sb(f"cV2{i}", [128, 2, 2, 130], BF16) for i in range(3)]
    Dd = [c.sb(f"cD{d}", [128, 130]) for d in range(2)]
    c.op("pool", lambda E: E.memset(Cb[:, 0, 0, :], 0.0), w=[Cb])
    c.op("pool", lambda E: E.memset(Cb[:, 1, NT - 1, :], 0.0), w=[Cb])
    vi = 0
    for i in range(NT):
        for d, tile, prev, nxt in ((0, i, i - 1, i + 1), (1, NT - 1 - i, NT - i, NT - 2 - i)):
            v2 = V2[vi % 3]
            vi += 1
            mk_v2(v2, tile)
            pu = self.ps()
            c.op("pe", lambda E: E.matmul(pu[:, 0:129], ktm[:, tile, 0, :], v2[:, d, 0, 0:129], start=True, stop=False), w=[pu], r=[ktm, v2])
            c.op("pe", lambda E: E.matmul(pu[:, 0:129], ktm[:, tile, 1, :], v2[:, d, 1, 0:129], start=False, stop=True), w=[pu], r=[ktm, v2])
            if i == 0:
                c.op("dve", lambda E: E.tensor_copy(Dd[d][:, 0:129], pu[:, 0:129]), w=[Dd[d]], r=[pu])
            else:
                c.op("dve", lambda E: E.scalar_tensor_tensor(Dd[d][:, 0:129], Dd[d][:, 0:129], eblp[:, prev, d:d + 1], pu[:, 0:129], ALU.mult, ALU.add), w=[Dd[d]], r=[Dd[d], eblp, pu])
            if 0 <= nxt < NT:
                c.op("act", lambda E: E.activation(Cb[:, d, nxt, 0:129], Dd[d][:, 0:129], AF.Copy, scale=eblp[:, tile, d:d + 1]), w=[Cb], r=[Dd[d], eblp])
    if CSTOP <= 3:
        return
    self.tm("C.pass1_done")
    oo, _ = cfg.col["C_o"]
    zo, _ = cfg.col["C_z"]
    self.load_w(wv, self.w_in_d[l][:, oo + P * 256:oo + (P + 1) * 256], 256)
    self.load_w(wz2, self.w_in_d[l][:, zo + P * 256:zo + (P + 1) * 256], 256)
    AT = [c.sb(f"cAT{i}", [128, 2, 2, 128], BF16) for i in range(3)]
    nsb = [c.sb(f"cns{i}", [128, 2, 2, 129]) for i in range(2)]
    dn = [c.sb(f"cdn{i}", [128, 2, 2]) for i in range(2)]
    hh4 = [c.sb(f"chh{i}", [128, 2, 2, 128]) for i in range(2)]
    hs_ = [c.sb(f"chs{i}", [128, 256]) for i in range(2)]
    so = [c.sb(f"cso{i}", [128, 256]) for i in range(2)]
    st6 = [c.sb(f"cst6{i}", [128, 2, 6]) for i in range(2)]
    mv = [c.sb(f"cmv{i}", [128, 2, 2]) for i in range(2)]
    rs = [c.sb(f"crs{i}", [128, 2]) for i in range(2)]
    ytm = [c.sb(f"cytm{i}", [128, 256], BF16) for i in range(2)]
    yT = [c.sb(f"cyT{i}", [128, 2, 128], BF16) for i in range(2)]
    go, _ = cfg.rp[("gC", l)]
    go, gw = go + P * 256, 256
    mo = cfg.cc["triU"][0]
    masks = self.cf[:, mo:mo + 256].rearrange("p (d n) -> p d n", d=2).unsqueeze(2).broadcast_to([128, 2, 2, 128])
    masks3 = self.cf[:, mo:mo + 256].rearrange("p (d n) -> p d n", d=2)

    def bufs(tile):
        return (AT[tile % 3], V2[tile % 3], nsb[tile % 2], dn[tile % 2], hh4[tile % 2], hs_[tile % 2], so[tile % 2],
                st6[tile % 2], mv[tile % 2], rs[tile % 2], ytm[tile % 2], yT[tile % 2])

    def s0(tile):
        tk = slice(tile * 128, (tile + 1) * 128)
        at, v2 = bufs(tile)[0:2]
        for h in range(2):
            rows = slice(h * 64, h * 64 + 64)
            pst_ = self.ps()
            c.op("pe", lambda E: E.matmul(pst_[:, 0:128], kc[rows, tk], qc[rows, tk], start=True, stop=True), w=[pst_], r=[kc, qc])
            c.op("dve", lambda E: E.tensor_tensor(at[:, :, h, :], pst_[:, 0:128].unsqueeze(1).broadcast_to([128, 2, 128]), masks3, ALU.mult), w=[at], r=[pst_, self.cf])
        mk_v2(v2, tile)

    def s1(tile):
        if CSTOP <= 4:
            return
        tk = slice(tile * 128, (tile + 1) * 128)
        at, v2, ns, dd, h4, a = bufs(tile)[0:6]
        for d in range(2):
            for h in range(2):
                rows = slice(h * 64, h * 64 + 64)
                j = d * 2 + h
                pn = self.ps()
                c.op("pe", lambda E: E.matmul(pn[:, 0:129], at[:, d, h, :], v2[:, d, h, 0:129], start=True, stop=False), w=[pn], r=[at, v2])
                c.op("pe", lambda E: E.matmul(pn[:, 0:129], qc[rows, tk], Cb[rows, d, tile, 0:129], start=False, stop=True), w=[pn], r=[qc, Cb])
                c.op("act", lambda E: E.activation(ns[:, d, h, :], pn[:, 0:129], AF.Copy, scale=eb8[:, tile, j:j + 1]), w=[ns], r=[pn, eb8])
        c.op("dve", lambda E: E.tensor_scalar(dd[:], ns[:, :, :, 128], -1.0, 1.0, ALU.mult, ALU.max), w=[dd], r=[ns])
        c.op("dve", lambda E: E.tensor_tensor(dd[:], dd[:], ns[:, :, :, 128], ALU.max), w=[dd], r=[dd, ns])
        c.op("dve", lambda E: E.reciprocal(dd[:], dd[:]), w=[dd], r=[dd])
        c.op("pool", lambda E: E.tensor_tensor(h4[:], ns[:, :, :, 0:128], dd[:].unsqueeze(3).broadcast_to([128, 2, 2, 128]), ALU.mult), w=[h4], r=[ns, dd])
        c.op("pool", lambda E: E.tensor_tensor(a[:].rearrange("p (h n) -> p h n", h=2), h4[:, 0], h4[:, 1], ALU.add), w=[a], r=[h4])

    def s2(tile):
        if CSTOP <= 5:
            return
        a, s_, s6, m2, r2, yt = bufs(tile)[5:11]
        ps = self.ps()
        self.proj_tm(ps, wv, 0, 256, tile)
        c.op("act", lambda E: E.activation(s_[:], ps[:, 0:256], AF.Sigmoid), w=[s_], r=[ps])
        c.op("dve", lambda E: E.tensor_tensor(a[:], a[:], s_[:], ALU.mult), w=[a], r=[a, s_])
        for h in range(2):
            c.op("dve", lambda E: E.bn_stats(s6[:, h, :], a[:, h * 128:(h + 1) * 128]), w=[s6], r=[a])
            c.op("dve", lambda E: E.bn_aggr(m2[:, h, :], s6[:, h, :]), w=[m2], r=[s6])
        c.op("pool", lambda E: E.tensor_scalar(r2[:], m2[:, :, 1], EPS, None, ALU.add), w=[r2], r=[m2])
        c.op("pool", lambda E: E.tensor_tensor(r2[:], r2[:], self.negh[:, 0:2], ALU.pow), w=[r2], r=[r2, self.negh])
        for h in range(2):
            c.op("dve", lambda E: E.tensor_scalar(a[:, h * 128:(h + 1) * 128], a[:, h * 128:(h + 1) * 128], m2[:, h, 0:1], r2[:, h:h + 1], ALU.subtract, ALU.mult), w=[a], r=[a, m2, r2])
        c.op("pool", lambda E: E.tensor_tensor(a[:], a[:], self.rowt[:, go:go + gw], ALU.mult), w=[a], r=[a, self.rowt])
        ps = self.ps()
        self.proj_tm(ps, wz2, 0, 256, tile)
        c.op("act", lambda E: E.activation(s_[:], ps[:, 0:256], AF.Sigmoid), w=[s_], r=[ps])
        c.op("pool", lambda E: E.tensor_tensor(a[:], a[:], s_[:], ALU.mult), w=[a], r=[a, s_])
        c.op("dve", lambda E: E.tensor_tensor(yt[:], ps[:, 0:256], a[:], ALU.mult), w=[yt], r=[ps, a])

    def s3(tile):
        if CSTOP <= 6:
            return
        tk = slice(tile * 128, (tile + 1) * 128)
        yt, yo = bufs(tile)[10:12]
        pT = self.pst()
        for h in range(2):
            c.op("pe", lambda E: E.transpose(pT[:, h * 128:(h + 1) * 128], yt[:, h * 128:(h + 1) * 128], ident), w=[pT], r=[yt, self.cb])
        c.op("act", lambda E: E.copy(yo[:], pT[:, 0:256].rearrange("p (h n) -> p h n", h=2)), w=[yo], r=[pT])
        c.dma("sp", self.yT_d[cfg.yC + P * 256:cfg.yC + (P + 1) * 256, tk].rearrange("(h p) n -> p h n", p=128), yo[:], w=[self.yT_d], r=[yo])

    for step in range(NT + 4):
        for fn, lag in ((s3, 4), (s2, 3), (s1, 2), (s0, 0)):
            if 0 <= step - lag < NT:
                fn(step - lag)


Gen.phase_C_pair = phase_C_pair
Gen.phase_C = phase_C


_CACHE = {}
_SELS = [([0, 1], [0], [0, 1], [0, 1]), ([2, 3], [1], [2, 3], [2, 3])]


def kernel(**inputs):
    inp = {k: np.asarray(v) for k, v in inputs.items()}
    B, T = inp["x"].shape[0], inp["x"].shape[1]
    cfg = Cfg(T, 2, 1, 2, 2)
    pks = [pack_weights(cfg, inp, *s) for s in _SELS]
    if "nc" not in _CACHE:
        _CACHE["nc"] = Gen(cfg, phases="NDBACO", mode="split").build()
    nc = _CACHE["nc"]
    in_maps = []
    for b in range(B):
        for h in range(2):
            m = dict(pks[h])
            m["x"] = np.ascontiguousarray(inp["x"][b], dtype=np.float32)
            m["mem"] = np.ascontiguousarray(inp["mem"][b], dtype=np.float32)
            m["pos"] = np.ascontiguousarray(inp["positions"][b:b + 1], dtype=np.int32)
            in_maps.append(m)
    res = run_bass_kernel_spmd(nc, in_maps, core_ids=list(range(2 * B)))
    return np.stack([np.asarray(res.results[2 * b]["out"], dtype=np.float32) for b in range(B)], axis=0)
```
